# Optimizing a Trainium2 kernel written in Bass

```python
import math
import jax, jax.numpy as jnp
from jax import lax
import numpy as np

D_MODEL = 1024
BATCH = 8
SEQ = 2048
DEPTH = 2

EPS = 1e-6
RET_HEADS = 4
RET_QK_DIM = 128
RET_V_DIM = 128
RET_WIDTH = RET_HEADS * RET_V_DIM
RET_CHUNK = 128
ROPE_THETA = 10000.0
SSM_WIDTH = D_MODEL
SSM_HEAD_DIM = 64
SSM_HEADS = SSM_WIDTH // SSM_HEAD_DIM
SSM_GROUPS = 2
SSM_STATE = 128
SSM_CONV = 4
SSM_CHUNK = 128
SSM_XBC = SSM_WIDTH + 2 * SSM_GROUPS * SSM_STATE
SC_WIDTH = D_MODEL // 2
SC_CONV = 3
N_BRANCH = 3

IN_SIZES = (
    RET_HEADS * RET_QK_DIM,
    RET_HEADS * RET_QK_DIM,
    RET_WIDTH,
    RET_WIDTH,
    SSM_WIDTH,
    SSM_XBC,
    SSM_HEADS,
    SC_WIDTH,
    SC_WIDTH,
    SC_WIDTH,
    SC_WIDTH,
    N_BRANCH * D_MODEL,
)
IN_WIDTH = sum(IN_SIZES)

kernel_name = "hybrid_retention_ssd_shortconv_gated_merge"


def rms_norm(x, g):
    xf = x.astype(jnp.float32)
    y = xf * lax.rsqrt(jnp.mean(xf * xf, axis=-1, keepdims=True) + EPS)
    return (y * g.astype(jnp.float32)).astype(x.dtype)


def causal_depthwise_conv(u, w):
    K = w.shape[0]
    L = u.shape[1]
    up = jnp.pad(u, ((0, 0), (K - 1, 0), (0, 0)))
    return sum(up[:, k:k + L] * w[k] for k in range(K))


def rotary(x, pos):
    dh = x.shape[-1]
    inv_freq = ROPE_THETA ** (-jnp.arange(0, dh, 2, dtype=jnp.float32) / dh)
    ang = pos.astype(jnp.float32)[..., None] * inv_freq
    cos = jnp.cos(ang)[:, :, None, :]
    sin = jnp.sin(ang)[:, :, None, :]
    x1, x2 = jnp.split(x, 2, axis=-1)
    return jnp.concatenate([x1 * cos - x2 * sin, x1 * sin + x2 * cos], axis=-1)


def chunkwise_retention(q, k, v):
    Bsz, L, H, dk = q.shape
    dv = v.shape[-1]
    C = RET_CHUNK
    n = L // C
    gamma = 1.0 - jnp.exp2(-5.0 - jnp.arange(H, dtype=jnp.float32))
    log_g = jnp.log(gamma)
    idx = jnp.arange(C, dtype=jnp.float32)
    diff = idx[:, None] - idx[None, :]
    decay_mask = jnp.exp(jnp.where(diff[None] >= 0, diff[None] * log_g[:, None, None], -jnp.inf))
    k_decay = jnp.exp((C - 1 - idx)[None, :] * log_g[:, None])
    q_decay = jnp.exp((idx + 1)[None, :] * log_g[:, None])
    chunk_decay = jnp.exp(C * log_g)

    def to_chunks(t):
        return t.reshape(Bsz, n, C, H, t.shape[-1]).transpose(1, 0, 3, 2, 4)

    qc, kc, vc = to_chunks(q), to_chunks(k), to_chunks(v)
    scores = jnp.einsum('nbhid,nbhjd->nbhij', qc, kc) * decay_mask[None, None]
    inner = jnp.einsum('nbhij,nbhje->nbhie', scores, vc)
    kv = jnp.einsum('nbhjd,hj,nbhje->nbhde', kc, k_decay, vc)

    def step(state, kv_n):
        return chunk_decay[None, :, None, None] * state + kv_n, state

    _, prev = lax.scan(step, jnp.zeros((Bsz, H, dk, dv), jnp.float32), kv)
    cross = jnp.einsum('nbhid,nbhde->nbhie', qc, prev) * q_decay[None, None, :, :, None]
    out = (inner + cross).transpose(1, 0, 3, 2, 4)
    return out.reshape(Bsz, L, H, dv)


def ssd_chunked(x, dt, A, Bm, Cm):
    b, l, h, p = x.shape
    g, nst = Bm.shape[-2], Bm.shape[-1]
    hg = h // g
    CH = SSM_CHUNK
    nc = l // CH
    xc = x.reshape(b, nc, CH, g, hg, p)
    dtc = dt.reshape(b, nc, CH, g, hg)
    Bc = Bm.reshape(b, nc, CH, g, nst)
    Cc = Cm.reshape(b, nc, CH, g, nst)
    a = (dtc * A.reshape(g, hg)).transpose(0, 3, 4, 1, 2)
    a_cs = jnp.cumsum(a, axis=-1)
    seg = a_cs[..., :, None] - a_cs[..., None, :]
    tril = jnp.tril(jnp.ones((CH, CH), dtype=bool))
    Lm = jnp.exp(jnp.where(tril, seg, -jnp.inf))
    xdt = xc * dtc[..., None]
    y_diag = jnp.einsum('bclgn,bcsgn,bghcls,bcsghp->bclghp', Cc, Bc, Lm, xdt)
    decay_states = jnp.exp(a_cs[..., -1:] - a_cs)
    states = jnp.einsum('bclgn,bghcl,bclghp->bcghpn', Bc, decay_states, xdt)
    chunk_decay = jnp.exp(a_cs[..., -1])

    def step(hstate, inp):
        s, d = inp
        return d[..., None, None] * hstate + s, hstate

    _, prev = lax.scan(step, jnp.zeros((b, g, hg, p, nst), jnp.float32),
                       (jnp.moveaxis(states, 1, 0), jnp.moveaxis(chunk_decay, 3, 0)))
    prev = jnp.moveaxis(prev, 0, 1)
    y_off = jnp.einsum('bclgn,bcghpn,bghcl->bclghp', Cc, prev, jnp.exp(a_cs))
    return (y_diag + y_off).reshape(b, l, h, p)


def retention_branch(q, k, v, gate, positions):
    Bsz, L, _ = q.shape
    f32 = jnp.float32
    qh = rotary(q.astype(f32).reshape(Bsz, L, RET_HEADS, RET_QK_DIM), positions) * (RET_QK_DIM ** -0.5)
    kh = rotary(k.astype(f32).reshape(Bsz, L, RET_HEADS, RET_QK_DIM), positions)
    vh = v.astype(f32).reshape(Bsz, L, RET_HEADS, RET_V_DIM)
    o = chunkwise_retention(qh, kh, vh)
    mu = jnp.mean(o, axis=-1, keepdims=True)
    var = jnp.mean(jnp.square(o - mu), axis=-1, keepdims=True)
    o = ((o - mu) * lax.rsqrt(var + EPS)).reshape(Bsz, L, RET_WIDTH)
    return (o * jax.nn.silu(gate.astype(f32))).astype(q.dtype)


def ssd_branch(z, xbc, dt_raw, conv_w, conv_b, dt_bias, a_log, d_skip, norm_g):
    Bsz, L, _ = z.shape
    f32 = jnp.float32
    xbc = jax.nn.silu(causal_depthwise_conv(xbc, conv_w) + conv_b)
    xs, Bm, Cm = jnp.split(xbc, [SSM_WIDTH, SSM_WIDTH + SSM_GROUPS * SSM_STATE], axis=-1)
    xs = xs.astype(f32).reshape(Bsz, L, SSM_HEADS, SSM_HEAD_DIM)
    Bm = Bm.astype(f32).reshape(Bsz, L, SSM_GROUPS, SSM_STATE)
    Cm = Cm.astype(f32).reshape(Bsz, L, SSM_GROUPS, SSM_STATE)
    dt = jax.nn.softplus(dt_raw.astype(f32) + dt_bias.astype(f32))
    A = -jnp.exp(a_log.astype(f32))
    y = ssd_chunked(xs, dt, A, Bm, Cm) + d_skip.astype(f32)[:, None] * xs
    y = y.reshape(Bsz, L, SSM_WIDTH) * jax.nn.silu(z.astype(f32))
    return rms_norm(y, norm_g).astype(z.dtype)


def short_conv_branch(bg, cg, hv, gate, conv_w):
    u = cg * hv
    return bg * causal_depthwise_conv(u, conv_w) * jax.nn.silu(gate)


def setup_inputs(seed: int = 0) -> dict:
    key = jax.random.key(seed)
    ks = jax.random.split(key, 24)
    D = D_MODEL
    nrm = jax.random.normal
    dt0 = jnp.exp(jax.random.uniform(ks[8], (DEPTH, SSM_HEADS)) * (math.log(0.1) - math.log(0.001)) + math.log(0.001))
    return {
        "x": nrm(ks[0], (BATCH, SEQ, D), jnp.float32),
        "c": nrm(ks[1], (BATCH, D), jnp.float32),
        "positions": (jnp.arange(SEQ, dtype=jnp.int32)[None, :]
                      + jax.random.randint(ks[2], (BATCH, 1), 0, 1024, dtype=jnp.int32)),
        "norm_g": 1.0 + 0.02 * nrm(ks[3], (DEPTH, D)),
        "w_ada": nrm(ks[4], (DEPTH, D, 3 * D)) * (0.5 * D ** -0.5),
        "b_ada": 0.02 * nrm(ks[5], (DEPTH, 3 * D)),
        "w_in": nrm(ks[6], (DEPTH, D, IN_WIDTH)) * D ** -0.5,
        "ssm_conv_w": nrm(ks[7], (DEPTH, SSM_CONV, SSM_XBC)) * SSM_CONV ** -0.5,
        "ssm_conv_b": 0.02 * nrm(ks[9], (DEPTH, SSM_XBC)),
        "ssm_dt_bias": dt0 + jnp.log(-jnp.expm1(-dt0)),
        "ssm_a_log": jnp.log(jax.random.uniform(ks[10], (DEPTH, SSM_HEADS), minval=1.0, maxval=16.0)),
        "ssm_d": 1.0 + 0.02 * nrm(ks[11], (DEPTH, SSM_HEADS)),
        "ssm_norm_g": 1.0 + 0.02 * nrm(ks[12], (DEPTH, SSM_WIDTH)),
        "sc_conv_w": nrm(ks[13], (DEPTH, SC_CONV, SC_WIDTH)) * SC_CONV ** -0.5,
        "w_br_ret": nrm(ks[14], (DEPTH, RET_WIDTH, D)) * RET_WIDTH ** -0.5,
        "w_br_ssm": nrm(ks[15], (DEPTH, SSM_WIDTH, D)) * SSM_WIDTH ** -0.5,
        "w_br_sc": nrm(ks[16], (DEPTH, SC_WIDTH, D)) * SC_WIDTH ** -0.5,
        "w_out": nrm(ks[17], (DEPTH, D, D)) * D ** -0.5,
        "final_norm_g": 1.0 + 0.02 * nrm(ks[18], (D,)),
    }


def reference(x, c, positions, norm_g, w_ada, b_ada, w_in, ssm_conv_w, ssm_conv_b, ssm_dt_bias,
              ssm_a_log, ssm_d, ssm_norm_g, sc_conv_w, w_br_ret, w_br_ssm, w_br_sc, w_out, final_norm_g):
    Bsz, L, D = x.shape
    split_idx = np.cumsum(IN_SIZES)[:-1].tolist()
    c_act = jax.nn.silu(c)
    for layer in range(DEPTH):
        ada = c_act @ w_ada[layer] + b_ada[layer]
        shift, scale, res_gate = jnp.split(ada, 3, axis=-1)
        h = rms_norm(x, norm_g[layer]) * (1.0 + scale[:, None]) + shift[:, None]
        proj = h @ w_in[layer]
        (r_q, r_k, r_v, r_g, s_z, s_xbc, s_dt,
         c_b, c_c, c_h, c_g, merge_logits) = jnp.split(proj, split_idx, axis=-1)
        y_ret = retention_branch(r_q, r_k, r_v, r_g, positions)
        y_ssm = ssd_branch(s_z, s_xbc, s_dt, ssm_conv_w[layer], ssm_conv_b[layer], ssm_dt_bias[layer],
                           ssm_a_log[layer], ssm_d[layer], ssm_norm_g[layer])
        y_sc = short_conv_branch(c_b, c_c, c_h, c_g, sc_conv_w[layer])
        gates = jax.nn.sigmoid(merge_logits).reshape(Bsz, L, N_BRANCH, D)
        merged = (gates[:, :, 0] * (y_ret @ w_br_ret[layer])
                  + gates[:, :, 1] * (y_ssm @ w_br_ssm[layer])
                  + gates[:, :, 2] * (y_sc @ w_br_sc[layer]))
        out = merged @ w_out[layer]
        x = x + res_gate[:, None] * out
    return rms_norm(x, final_norm_g)
```

```python
import math
from contextlib import ExitStack

import numpy as np
import ml_dtypes
import concourse.bass as bass
import concourse.mybir as mybir
from concourse.bass_utils import run_bass_kernel_spmd

F32 = mybir.dt.float32
BF16 = mybir.dt.bfloat16
I32 = mybir.dt.int32
AF = mybir.ActivationFunctionType
ALU = mybir.AluOpType
AX = mybir.AxisListType

ENGS = ["pe", "act", "dve", "pool", "sp"]
N_DMA_SEMS = 36
DMA_SEM_POOLS = {"sp": list(range(0, 12)), "pool": list(range(12, 36))}

D = 1024
L = 2048
DEPTH = 2
NQ = 4
NCQ = 4
EPS = 1e-6
IN_W = 9744
O_Q, O_K, O_V, O_G, O_Z, O_XBC, O_DT, O_CB = 0, 512, 1024, 1536, 2048, 3072, 4608, 4624
O_G0, O_G1, O_G2 = 6672, 7696, 8720
NP_SLOT = 24
SLOT_COLS = 8 * 2576

C_U, C_ONES, C_CMASK, C_MASKT, C_QDEC, C_KDEC, C_INVF, C_NHALF, C_TWOPI = (
    0, 128, 256, 384, 896, 1408, 1412, 1476, 1492)
C_TOT = 1496


class Plan:
    def __init__(self):
        self.ops = {e: [] for e in ENGS}
        self.count = {e: 0 for e in ENGS}
        self.waited = {e: {} for e in ENGS}
        self.last_write = {}
        self.readers = {}
        self.dma_count = [0] * N_DMA_SEMS
        self.dma_next = {k: 0 for k in DMA_SEM_POOLS}
        self.enabled = True
        self.stop_at = None

    def mark(self, name):
        if self.stop_at is not None and name == self.stop_at:
            self.enabled = False

    def _deps(self, eng, reads, writes, extra=()):
        deps = {}

        def add(tok):
            s, v = tok
            if deps.get(s, 0) < v:
                deps[s] = v

        for k in reads:
            if k in self.last_write:
                add(self.last_write[k])
        for k in writes:
            if k in self.last_write:
                add(self.last_write[k])
            for s, v in self.readers.get(k, {}).items():
                add((s, v))
        for t in extra:
            add(t)
        waits = []
        for s, v in deps.items():
            if s == eng and eng == "pe":
                continue
            if self.waited[eng].get(s, 0) < v:
                waits.append((s, v))
                self.waited[eng][s] = v
        return waits

    def _record(self, tok, reads, writes):
        for k in writes:
            self.last_write[k] = tok
            self.readers[k] = {}
        for k in reads:
            d = self.readers.setdefault(k, {})
            if d.get(tok[0], 0) < tok[1]:
                d[tok[0]] = tok[1]

    @staticmethod
    def _excl(reads, writes):
        ps = [k for k in reads if isinstance(k, tuple) and k[0] == "ps"]
        if ps:
            reads = [k for k in reads if not (isinstance(k, tuple) and k[0] == "ps")]
            writes = list(writes) + [k for k in ps if k not in writes]
        return reads, writes

    def op(self, eng, fn, reads=(), writes=()):
        if not self.enabled:
            return None
        reads, writes = self._excl(reads, writes)
        waits = self._deps(eng, reads, writes)
        self.count[eng] += 1
        tok = (eng, self.count[eng])
        self.ops[eng].append((waits, fn, tok, 1))
        self._record(tok, reads, writes)
        return tok

    def dma(self, eng, fn, reads=(), writes=(), force=False):
        if not self.enabled and not force:
            return None
        pool_ = DMA_SEM_POOLS[eng]
        i = pool_[self.dma_next[eng]]
        self.dma_next[eng] = (self.dma_next[eng] + 1) % len(pool_)
        s = ("dma", i)
        extra = []
        if self.dma_count[i] > 0:
            extra.append((s, self.dma_count[i]))
        waits = self._deps(eng, reads, writes, extra)
        self.dma_count[i] += 16
        tok = (s, self.dma_count[i])
        self.ops[eng].append((waits, fn, tok, 16))
        self._record(tok, reads, writes)
        return tok

    def finish(self, eng, toks):
        waits = []
        for s, v in toks:
            if self.waited[eng].get(s, 0) < v:
                waits.append((s, v))
                self.waited[eng][s] = v
        self.ops[eng].append((waits, None, None, 0))

    def emit(self, block, sems):
        def run(eng_name):
            def body(e):
                for waits, fn, tok, inc in self.ops[eng_name]:
                    for s, v in waits:
                        e.wait_ge(sems[s], v)
                    if fn is None:
                        continue
                    ins = fn(e)
                    ins.then_inc(sems[tok[0]], inc)
            return body

        block.tensor(run("pe"))
        block.scalar(run("act"))
        block.vector(run("dve"))
        block.gpsimd(run("pool"))
        block.sync(run("sp"))


def build_program(dbg=False, n_quarters=NQ, n_layers=DEPTH, stop_at=None):
    nc = bass.Bass("TRN2", target_bir_lowering=False)

    def din(name, shape, dt=F32):
        return nc.dram_tensor(name, list(shape), dt, kind="ExternalInput").ap()

    x_d = din("x", [L, D])
    c_d = din("c", [128, 8])
    pos_d = din("pos", [128, 16], I32)
    w_in_d = din("w_in", [DEPTH, D, IN_W])
    w_ada_d = din("w_ada", [DEPTH, D, 3 * D])
    w_brr_d = din("w_br_ret", [DEPTH, 512, D])
    w_brs_d = din("w_br_ssm", [DEPTH, D, D])
    w_brc_d = din("w_br_sc", [DEPTH, 512, D])
    w_out_d = din("w_out", [DEPTH, D, D])
    ng_d = din("norm_g_fm", [DEPTH, 128, 8])
    bada_d = din("b_ada_bc", [DEPTH, 128, 3 * D])
    cw_d = din("conv_w_fm", [DEPTH, 128, 48])
    cb_d = din("conv_b_row", [DEPTH, 1, 1536])
    dtb_d = din("dt_bias_bc", [DEPTH, 128, 16])
    alog_d = din("a_log_bc", [DEPTH, 128, 16])
    dfm_d = din("d_fm", [DEPTH, 128, 8])
    sng_d = din("ssm_norm_g_bc", [DEPTH, 128, D])
    scw_d = din("sc_w_fm", [DEPTH, 128, 12])
    fg_d = din("final_g_bc", [128, D])
    cf_d = din("cf32", [128, C_TOT])
    cb16_d = din("cbf16", [128, 256], BF16)
    out_d = nc.dram_tensor("out", [L, D], F32, kind="ExternalOutput").ap()
    dbg_d = {}

    P = Plan()
    P.stop_at = stop_at
    with ExitStack() as st:
        sems = {}
        for e in ENGS:
            sems[e] = st.enter_context(nc.semaphore("s_" + e))
        for i in range(N_DMA_SEMS):
            sems[("dma", i)] = st.enter_context(nc.semaphore("s_dma%d" % i))

        def sb(name, cols, dt=F32, parts=128):
            return st.enter_context(nc.sbuf_tensor(name, [parts, cols], dt))

        def V3(ap, a):
            return ap.rearrange("p (a b) -> p a b", a=a)

        def V4(ap, a, b):
            return ap.rearrange("p (a b c) -> p a b c", a=a, b=b)

        xq = sb("xq", NCQ * D)
        hT = sb("hT", 8 * 512, BF16)
        merged = sb("merged", NCQ * D, BF16)
        ytemp = sb("ytemp", 8 * 512, BF16)
        slots = [sb("wslot0", SLOT_COLS, BF16), sb("wslot1", SLOT_COLS, BF16)]
        ret_f = [sb("ret_f%d" % l, 512) for l in range(DEPTH)]
        ssd_f = [sb("ssd_f%d" % l, 1024) for l in range(DEPTH)]
        ret_bf = sb("ret_bf", 512, BF16)
        ssd_bf = sb("ssd_bf", 1024, BF16)
        xtail = [sb("xtail%d" % l, 36, BF16) for l in range(DEPTH)]
        utail = [sb("utail%d" % l, 8, BF16) for l in range(DEPTH)]
        cf = sb("cf", C_TOT)
        cb16 = sb("cb16", 256, BF16)
        rg_bc = [sb("rg_bc%d" % l, D) for l in range(DEPTH)]
        gbc = sb("gbc", D)
        cossin = sb("cossin", 512)
        gs = [sb("gs%d" % l, 8) for l in range(DEPTH)]
        sh = [sb("sh%d" % l, 8) for l in range(DEPTH)]
        A_bc = [sb("A_bc%d" % l, 16) for l in range(DEPTH)]
        dtb = [sb("dtb%d" % l, 16) for l in range(DEPTH)]
        cw = [sb("cw%d" % l, 48) for l in range(DEPTH)]
        scw = [sb("scw%d" % l, 12) for l in range(DEPTH)]
        dfm = [sb("dfm%d" % l, 8) for l in range(DEPTH)]
        ngf = [sb("ngf%d" % l, 8) for l in range(DEPTH)]
        cbrow = sb("cbrow", 1536, BF16, parts=1)
        onesrow = sb("onesrow", 128, BF16, parts=1)
        diagw = sb("diagw", 48 * 128, BF16)
        diagsc = diagw
        dmt = sb("dmt", 8 * 128, BF16)
        posf = sb("posf", 16)
        cact = sb("cact", 8)
        small = sb("small", 512)
        W = [sb("W4_%d" % i, 1024) for i in range(6)]
        xpad = sb("xpad", 12 * 131, BF16)
        xc = sb("xc", 12 * 128, BF16)
        misc1 = sb("misc1", 512, BF16)
        ps = st.enter_context(nc.psum_tensor("ps", [128, 4096], F32))
        block = st.enter_context(nc.Block())

        def bank(b, n=1):
            return ps[:, b * 512:(b + n) * 512]

        def bankb(b):
            return ps[:, b * 512:(b + 1) * 512].bitcast(BF16)

        def PK(*bs):
            return [("ps", b) for b in bs]

        def SK(s):
            return [("slot", s, j) for j in range(NP_SLOT)]

        def wk(i, a=0, b=1024):
            return [("W", i, gq) for gq in range(a // 256, (b + 255) // 256)]

        identb = cb16[:, 0:128]
        xgt = cb16[:, 128:256]
        Uf = cf[:, C_U:C_U + 128]
        onesf = cf[:, C_ONES:C_ONES + 128]
        cmask = cf[:, C_CMASK:C_CMASK + 128]
        maskT = cf[:, C_MASKT:C_MASKT + 512]
        qdecT = cf[:, C_QDEC:C_QDEC + 512]
        kdec = cf[:, C_KDEC:C_KDEC + 4]
        invf = cf[:, C_INVF:C_INVF + 64]
        nhalf = cf[:, C_NHALF:C_NHALF + 16]
        twopi = cf[:, C_TWOPI:C_TWOPI + 1].to_broadcast([128, 256])

        dbg_list = []

        def dump(name, ap, cols, dt=F32, reads=()):
            if not dbg or not P.enabled:
                return
            t = nc.dram_tensor("dbg_" + name, [128, cols], dt, kind="ExternalOutput").ap()
            tok = P.dma("sp", lambda e: e.dma_start(out=t, in_=ap), reads=list(reads))
            dbg_list.append(tok)

        P.dma("sp", lambda e: e.dma_start(out=cf[:, :], in_=cf_d), writes=["cf"])
        P.dma("sp", lambda e: e.dma_start(out=cb16[:, :], in_=cb16_d), writes=["cb16"])
        ctmp = W[0][:, 0:8]
        P.dma("sp", lambda e: e.dma_start(out=ctmp, in_=c_d), writes=[*wk(0)])
        posi = W[1][:, 0:16].bitcast(I32)
        P.dma("sp", lambda e: e.dma_start(out=posi, in_=pos_d), writes=[*wk(1)])
        P.mark("s1")
        for l in range(DEPTH):
            for (t_, d_, nm) in ((A_bc[l], alog_d[l], "A_bc"), (dtb[l], dtb_d[l], "dtb"), (cw[l], cw_d[l], "cw"),
                                 (scw[l], scw_d[l], "scw"), (dfm[l], dfm_d[l], "dfm"), (ngf[l], ng_d[l], "ngf")):
                P.dma("sp", lambda e, t_=t_, d_=d_: e.dma_start(out=t_[:, :], in_=d_), writes=[(nm, l)])
            P.op("act", lambda e, l=l: e.activation(out=A_bc[l][:, :], in_=A_bc[l][:, :], func=AF.Exp),
                 reads=[("A_bc", l)], writes=[("A_bc", l)])
            P.op("dve", lambda e, l=l: e.tensor_scalar(out=A_bc[l][:, :], in0=A_bc[l][:, :], scalar1=-1.0, scalar2=None,
                                                       op0=ALU.mult), reads=[("A_bc", l)], writes=[("A_bc", l)])
            P.op("dve", lambda e, l=l: e.tensor_scalar(out=scw[l][:, :], in0=scw[l][:, :], scalar1=0.5, scalar2=None,
                                                       op0=ALU.mult), reads=[("scw", l)], writes=[("scw", l)])
            P.op("dve", lambda e, l=l: e.tensor_scalar(out=dfm[l][:, :], in0=dfm[l][:, :], scalar1=0.5, scalar2=None,
                                                       op0=ALU.mult), reads=[("dfm", l)], writes=[("dfm", l)])
            for t_, nm in ((ret_f[l], "ret_f"), (ssd_f[l], "ssd_f")):
                P.op("dve", lambda e, t_=t_: e.memset(t_[:, :], 0.0), writes=[(nm, l)])
            P.op("dve", lambda e, l=l: e.memset(xtail[l][:, :], 0.0), writes=[("xtail", l)])
            P.op("dve", lambda e, l=l: e.memset(utail[l][:, :], 0.0), writes=[("utail", l)])
        P.mark("s2")
        P.op("dve", lambda e: e.memset(onesrow[:, :], 1.0), writes=["onesrow"])
        P.op("dve", lambda e: e.tensor_copy(out=posf[:, :], in_=posi), reads=[*wk(1)], writes=["posf"])
        tct = W[0][:, 8:16]
        hct = W[0][:, 16:24]
        P.op("act", lambda e: e.activation(out=tct, in_=ctmp, func=AF.Tanh, scale=0.5), reads=[*wk(0)], writes=[*wk(0,0,256)])
        P.op("dve", lambda e: e.tensor_scalar(out=hct, in0=ctmp, scalar1=0.5, scalar2=None, op0=ALU.mult),
             reads=[*wk(0)], writes=[*wk(0,0,256)])
        P.op("dve", lambda e: e.scalar_tensor_tensor(out=cact[:, :], in0=tct, scalar=1.0, in1=hct, op0=ALU.add,
                                                     op1=ALU.mult), reads=[*wk(0,0,256), *wk(0,0,256)], writes=["cact"])
        crep = W[2][:, :]
        P.op("dve", lambda e: e.tensor_copy(out=V3(crep, 8), in_=cact[:, :].unsqueeze(2).to_broadcast([128, 8, 128])),
             reads=["cact"], writes=[*wk(2)])

        P.mark("s3")
        for l in range(DEPTH):
            for n in range(6):
                s = n % 2
                wv = V3(slots[s][:, 0:8192].bitcast(F32), 8)
                wsrc = w_ada_d[l][:, n * 512:(n + 1) * 512].rearrange("(k p) c -> p k c", p=128)
                for kk in range(8):
                    P.dma("sp", lambda e, wv=wv, wsrc=wsrc, kk=kk: e.dma_start(out=wv[:, kk, :], in_=wsrc[:, kk, :]),
                          writes=[("slot", s, kk)])
                bb = W[3][:, 0:512]
                P.dma("sp", lambda e, l=l, n=n, bb=bb: e.dma_start(out=bb, in_=bada_d[l][:, n * 512:(n + 1) * 512]),
                      writes=[*wk(3)])
                pb = n % 2

                def mm(e, wv=wv, pb=pb):
                    for kk in range(8):
                        ins = e.matmul(bank(pb), lhsT=V3(crep, 8)[:, kk, :], rhs=wv[:, kk, :], start=(kk == 0),
                                       stop=(kk == 7))
                    return ins
                P.op("pe", mm, reads=[*wk(2)] + SK(s), writes=PK(pb))
                P.mark("ada_mm_%d_%d" % (l, n))
                if n >= 4:
                    dst = rg_bc[l][:, (n - 4) * 512:(n - 3) * 512]
                    P.op("dve", lambda e, dst=dst, pb=pb, bb=bb: e.tensor_tensor(out=dst, in0=bank(pb), in1=bb, op=ALU.add),
                         reads=PK(pb) + [*wk(3)], writes=[("rg_bc", l)])
                    P.op("dve", lambda e, dst=dst: e.tensor_scalar(out=dst, in0=dst, scalar1=0.5, scalar2=None, op0=ALU.mult),
                         reads=[("rg_bc", l)], writes=[("rg_bc", l)])
                else:
                    blk = W[4][:, 0:512]
                    P.op("dve", lambda e, blk=blk, pb=pb, bb=bb: e.tensor_tensor(out=blk, in0=bank(pb), in1=bb, op=ALU.add),
                         reads=PK(pb) + [*wk(3)], writes=[*wk(4)])
                    P.op("dve", lambda e, blk=blk: e.tensor_tensor(out=V3(blk, 4), in0=V3(blk, 4),
                                                                   in1=identb.unsqueeze(1).to_broadcast([128, 4, 128]),
                                                                   op=ALU.mult), reads=[*wk(4), "cb16"], writes=[*wk(4)])
                    tgt = (sh[l] if n < 2 else gs[l])[:, (n % 2) * 4:(n % 2) * 4 + 4]
                    P.op("dve", lambda e, blk=blk, tgt=tgt: e.tensor_reduce(out=tgt, in_=V3(blk, 4), axis=AX.X, op=ALU.add),
                         reads=[*wk(4)], writes=[("shgs", l)])
            P.op("dve", lambda e, l=l: e.scalar_tensor_tensor(out=gs[l][:, :], in0=gs[l][:, :], scalar=1.0, in1=ngf[l][:, :],
                                                              op0=ALU.add, op1=ALU.mult),
                 reads=[("shgs", l), ("ngf", l)], writes=[("shgs", l)])
        P.mark("adaend")
        if dbg:
            dump("gs0", gs[0][:, :], 8, reads=[("shgs", 0)])
            dump("sh0", sh[0][:, :], 8, reads=[("shgs", 0)])
            dump("rg0", rg_bc[0][:, :], D, reads=[("rg_bc", 0)])

        P.mark("setup")
        GROUPS = ["A1", "A2", "B1", "B2", "C1", "C2"]
        seq = [(q, l, g) for q in range(n_quarters) for l in range(n_layers) for g in GROUPS]
        slot_of = {key: i % 2 for i, key in enumerate(seq)}

        GIDX = {g_: i_ for i_, g_ in enumerate(GROUPS)}
        USED = {"A1": 8 * 2048, "A2": 8192 + 4096, "B1": 8 * 2576, "B2": 16384, "C1": 16384, "C2": 12288 + 8192}
        wscr = nc.dram_tensor("wscr", [DEPTH * 6 * 128, SLOT_COLS], BF16).ap()

        def load_group(q, l, g):
            s = slot_of[(q, l, g)]
            sl = slots[s]
            used = USED[g]
            r0 = (l * 6 + GIDX[g]) * 128
            scr = wscr[r0:r0 + 128, :]
            if q > 0:
                nsp = 4
                w_ = used // nsp
                for j in range(nsp):
                    P.dma("sp", lambda e, j=j: e.dma_start(out=sl[:, j * w_:(j + 1) * w_], in_=scr[:, j * w_:(j + 1) * w_]),
                          reads=[("scr", l, g)], writes=[("slot", s, j)])
                return s
            pieces = []

            def add(col0, src, nk, cols):
                dst = sl[:, col0:col0 + nk * cols].rearrange("p (k c) -> p k c", k=nk)
                pieces.append((dst, src.rearrange("(k p) c -> p k c", p=128), nk))
            if g == "A1":
                add(0, w_in_d[l][:, 0:2048], 8, 2048)
            elif g == "A2":
                add(0, w_in_d[l][:, O_G0:O_G0 + 1024], 8, 1024)
                add(8192, w_brr_d[l], 4, 1024)
            elif g == "B1":
                add(0, w_in_d[l][:, O_Z:O_Z + 2576], 8, 2576)
            elif g == "B2":
                add(0, w_in_d[l][:, O_G1:O_G1 + 1024], 8, 1024)
                add(8192, w_brs_d[l], 8, 1024)
            elif g == "C1":
                add(0, w_in_d[l][:, O_CB:O_CB + 2048], 8, 2048)
            elif g == "C2":
                add(0, w_in_d[l][:, O_G2:O_G2 + 1024], 8, 1024)
                add(8192, w_brc_d[l], 4, 1024)
                add(12288, w_out_d[l], 8, 1024)
            j = 0
            for dst, src, nk in pieces:
                for kk in range(nk):
                    P.dma("pool", lambda e, dst=dst, src=src, kk=kk: e.dma_start(out=dst[:, kk, :], in_=src[:, kk, :]),
                          writes=[("slot", s, j)])
                    j += 1
            if n_quarters > 1:
                P.dma("sp", lambda e: e.dma_start(out=scr[:, 0:used], in_=sl[:, 0:used]),
                      reads=[("slot", s, jj) for jj in range(j)], writes=[("scr", l, g)])
            return s

        def prefetch_next(q, l, g, early=False):
            i = seq.index((q, l, g))
            if i + 1 >= len(seq):
                return
            nq = seq[i + 1][0]
            if g in ("A2", "B2", "C2"):
                if early != (nq > 0):
                    return
            load_group(*seq[i + 1])

        def rsqrt_small(dst, src, n, rk, wk):
            P.op("pool", lambda e: e.tensor_tensor(out=dst, in0=src, in1=nhalf[:, 0:n], op=ALU.pow),
                 reads=list(rk) + ["cf"], writes=list(wk))

        out_toks = []
        first_group_loaded = False
        xq3 = V3(xq[:, :], NCQ)
        for q in range(n_quarters):
            if q == 0:
                for t in range(NCQ):
                    r0 = t * 128
                    P.dma("sp", lambda e, t=t, r0=r0: e.dma_start(out=xq3[:, t, :], in_=x_d[r0:r0 + 128, :]),
                          writes=[("xq", t)])
                load_group(*seq[0])

            def final_cb(t, q=q):
                fssq, fms, frs = small[:, 44 + t:45 + t], small[:, 48 + t:49 + t], small[:, 52 + t:53 + t]
                P.op("act", lambda e: e.activation(out=xpad[:, 0:1024], in_=xq3[:, t, :], func=AF.Square, accum_out=fssq),
                     reads=[("xq", t)], writes=["xpad_h", ("xpad_b", 0), ("xpad_b", 1), ("xpad_b", 2), ("fssq", t)])
                P.op("dve", lambda e: e.tensor_scalar(out=fms, in0=fssq, scalar1=1.0 / D, scalar2=EPS, op0=ALU.mult, op1=ALU.add),
                     reads=[("fssq", t)], writes=[("fms", t)])
                rsqrt_small(frs, fms, 1, [("fms", t)], [("frs", t)])
                ob = W[4 + (t % 2)]
                P.op("dve", lambda e: e.scalar_tensor_tensor(out=ob[:, :], in0=xq3[:, t, :], scalar=frs, in1=gbc[:, :],
                                                             op0=ALU.mult, op1=ALU.mult),
                     reads=[("xq", t), ("frs", t), "gbc"], writes=wk(4 + (t % 2)))
                r0 = q * 512 + t * 128
                tok = P.dma("sp", lambda e: e.dma_start(out=out_d[r0:r0 + 128, :], in_=ob[:, :]), reads=wk(4 + (t % 2)))
                out_toks.append(tok)
                if q + 1 < n_quarters:
                    r1 = (q + 1) * 512 + t * 128
                    P.dma("sp", lambda e: e.dma_start(out=xq3[:, t, :], in_=x_d[r1:r1 + 128, :]), writes=[("xq", t)])

            for l in range(n_layers):
                emit_layer_quarter(nc, P, locals(), q, l)

        P.finish("sp", [t_ for t_ in out_toks + dbg_list if t_ is not None])
        P.emit(block, sems)
    return nc


def emit_layer_quarter(nc, P, env, q, l):
    g = env
    (xq, hT, merged, ytemp, slots, ret_f, ssd_f, ret_bf, ssd_bf, xtail, utail, cf, cb16, cossin, rg_bc, gbc, gs, sh,
     A_bc, dtb, cw, scw, dfm, cbrow, onesrow, diagw, diagsc, dmt, small, W, xpad, xc, misc1, ps) = [
        g[k] for k in ("xq", "hT", "merged", "ytemp", "slots", "ret_f", "ssd_f", "ret_bf", "ssd_bf", "xtail", "utail",
                       "cf", "cb16", "gbc", "rg_bc", "gbc", "gs", "sh", "A_bc", "dtb", "cw", "scw", "dfm", "cbrow",
                       "onesrow", "diagw", "diagsc", "dmt", "small", "W", "xpad", "xc", "misc1", "ps")]
    cossin = g["cossin"]
    bank, bankb, PK, SK, V3, V4 = g["bank"], g["bankb"], g["PK"], g["SK"], g["V3"], g["V4"]
    identb, xgt, Uf, onesf, cmask, maskT, qdecT, kdec, nhalf = [g[k] for k in (
        "identb", "xgt", "Uf", "onesf", "cmask", "maskT", "qdecT", "kdec", "nhalf")]
    slot_of, prefetch_next, rsqrt_small, dump, dbg, sng_d = (g["slot_of"], g["prefetch_next"], g["rsqrt_small"],
                                                             g["dump"], g["dbg"], g["sng_d"])
    wk = g["wk"]
    HTK = ["hT"] + [("hTk", k_) for k_ in range(8)]
    D3 = dbg and q == 0 and l == 0
    xq3 = V3(xq[:, :], NCQ)
    hT3 = V3(hT[:, :], 8)
    mg3 = V3(merged[:, :], NCQ)
    yT3 = V3(ytemp[:, :], 8)
    xn3 = V3(ytemp[:, :], NCQ)

    def emit_rotary(qq):
        invf = g["invf"]
        twopi = g["twopi"]
        posf = g["posf"]
        ang = W[5][:, 0:256]
        a1 = W[5][:, 256:512]
        a2 = W[5][:, 512:768]
        P.op("dve", lambda e: e.tensor_tensor(
            out=V3(ang, 4), in0=posf[:, qq * 4:(qq + 1) * 4].unsqueeze(2).to_broadcast([128, 4, 64]),
            in1=invf.unsqueeze(1).to_broadcast([128, 4, 64]), op=ALU.mult), reads=["posf", "cf"], writes=wk(5, 0, 256))
        MAGIC = 12582912.0
        C1 = 6.28125
        C2 = 2.0 * math.pi - 6.28125
        PI_LO = 3.1415925
        kf = W[5][:, 768:1024]
        P.op("dve", lambda e: e.tensor_scalar(out=kf, in0=ang, scalar1=1.0 / (2.0 * math.pi), scalar2=MAGIC, op0=ALU.mult,
                                              op1=ALU.add), reads=wk(5, 0, 256), writes=wk(5, 768, 1024))
        P.op("dve", lambda e: e.tensor_scalar(out=kf, in0=kf, scalar1=-MAGIC, scalar2=None, op0=ALU.add),
             reads=wk(5, 768, 1024), writes=wk(5, 768, 1024))
        P.op("dve", lambda e: e.scalar_tensor_tensor(out=a1, in0=kf, scalar=-C1, in1=ang, op0=ALU.mult, op1=ALU.add),
             reads=wk(5, 768, 1024) + wk(5, 0, 256), writes=wk(5, 256, 512))
        P.op("dve", lambda e: e.scalar_tensor_tensor(out=a1, in0=kf, scalar=-C2, in1=a1, op0=ALU.mult, op1=ALU.add),
             reads=wk(5, 768, 1024) + wk(5, 256, 512), writes=wk(5, 256, 512))
        P.op("dve", lambda e: e.tensor_scalar(out=a2, in0=a1, scalar1=0.5 * math.pi, scalar2=None, op0=ALU.add),
             reads=wk(5, 256, 512), writes=wk(5, 512, 768))
        P.op("dve", lambda e: e.tensor_scalar(out=kf, in0=a2, scalar1=math.pi, scalar2=-2.0 * math.pi, op0=ALU.is_gt,
                                              op1=ALU.mult), reads=wk(5, 512, 768), writes=wk(5, 768, 1024))
        P.op("dve", lambda e: e.tensor_tensor(out=a2, in0=a2, in1=kf, op=ALU.add), reads=wk(5, 512, 768) + wk(5, 768, 1024),
             writes=wk(5, 512, 768))
        P.op("dve", lambda e: e.tensor_scalar(out=a1, in0=a1, scalar1=-PI_LO, scalar2=PI_LO, op0=ALU.max, op1=ALU.min),
             reads=wk(5, 256, 512), writes=wk(5, 256, 512))
        P.op("dve", lambda e: e.tensor_scalar(out=a2, in0=a2, scalar1=-PI_LO, scalar2=PI_LO, op0=ALU.max, op1=ALU.min),
             reads=wk(5, 512, 768), writes=wk(5, 512, 768))
        P.op("act", lambda e: e.activation(out=cossin[:, 256:512], in_=a1, func=AF.Sin), reads=wk(5, 256, 512), writes=["cossin"])
        P.op("act", lambda e: e.activation(out=cossin[:, 0:256], in_=a2, func=AF.Sin), reads=wk(5, 512, 768), writes=["cossin"])
        if D3:
            dump("cossin", cossin[:, 0:512], 512, reads=["cossin"])

    def emit_prep(ll):
        P.dma("pool", lambda e: e.dma_start(out=cbrow[:, :], in_=g["cb_d"][ll]), writes=["cbrow"])

        dg4 = V4(diagw[:, :], 12, 4)
        P.op("pool", lambda e: e.tensor_tensor(out=V3(diagw[:, :], 48), in0=identb.unsqueeze(1).to_broadcast([128, 48, 128]),
                                               in1=cw[ll][:, :].unsqueeze(2).to_broadcast([128, 48, 128]), op=ALU.mult),
             reads=["cb16", ("cw", ll)], writes=["diagw"])
        dm3 = V3(dmt[:, :], 8)
        P.op("pool", lambda e: e.tensor_tensor(out=dm3, in0=identb.unsqueeze(1).to_broadcast([128, 8, 128]),
                                               in1=dfm[ll][:, :].unsqueeze(2).to_broadcast([128, 8, 128]), op=ALU.mult),
             reads=["cb16", ("dfm", ll)], writes=["dmt"])

    dg4 = V4(diagw[:, :], 12, 4)
    dm3 = V3(dmt[:, :], 8)
    n_layers_, n_quarters_ = g["n_layers"], g["n_quarters"]
    if q == 0 and l == 0:
        emit_rotary(0)
        emit_prep(0)
    ssq = small[:, 0:4]
    ms = small[:, 4:8]
    rstd = small[:, 8:12]
    for t in range(NCQ):
        P.op("act", lambda e, t=t: e.activation(out=W[0][:, 0:512].bitcast(BF16), in_=xq3[:, t, :], func=AF.Square,
                                                accum_out=ssq[:, t:t + 1]), reads=[("xq", t)], writes=[*wk(0), ("ssq", t)])
    for (c0, c1, tag) in ((0, NCQ - 1, "A"), (NCQ - 1, NCQ, "B")):
        P.op("dve", lambda e, c0=c0, c1=c1: e.tensor_scalar(out=ms[:, c0:c1], in0=ssq[:, c0:c1], scalar1=1.0 / D, scalar2=EPS,
                                                            op0=ALU.mult, op1=ALU.add),
             reads=[("ssq", t) for t in range(c0, c1)], writes=["ms" + tag])
        rsqrt_small(rstd[:, c0:c1], ms[:, c0:c1], c1 - c0, ["ms" + tag], ["rstd" + tag])
        for t in range(c0, c1):
            P.op("dve", lambda e, t=t: e.tensor_scalar(out=xn3[:, t, :], in0=xq3[:, t, :], scalar1=rstd[:, t:t + 1],
                                                       scalar2=None, op0=ALU.mult), reads=[("xq", t), "rstd" + tag], writes=["ytemp"])
    for kk in range(8):
        pb = kk % 2

        def tr(e, kk=kk, pb=pb):
            for t in range(NCQ):
                ins = e.transpose(out=bankb(pb)[:, t * 128:(t + 1) * 128], in_=xn3[:, t, kk * 128:(kk + 1) * 128],
                                  identity=identb)
            return ins
        P.op("pe", tr, reads=["ytemp", "cb16"], writes=PK(pb))
        if pb == 0:
            P.op("act", lambda e, kk=kk, pb=pb: e.activation(out=hT3[:, kk, :], in_=bankb(pb)[:, 0:512], func=AF.Identity,
                                                              scale=gs[l][:, kk:kk + 1], bias=sh[l][:, kk:kk + 1]),
                 reads=PK(pb) + [("shgs", l)], writes=[("hTk", kk)])
        else:
            P.op("dve", lambda e, kk=kk, pb=pb: e.tensor_scalar(out=hT3[:, kk, :], in0=bankb(pb)[:, 0:512],
                                                                scalar1=gs[l][:, kk:kk + 1], scalar2=sh[l][:, kk:kk + 1],
                                                                op0=ALU.mult, op1=ALU.add),
                 reads=PK(pb) + [("shgs", l)], writes=[("hTk", kk)])
    if D3:
        dump("hT", hT[:, :], 4096, BF16, reads=HTK)

    P.op("act", lambda e: e.activation(out=ret_bf[:, :], in_=ret_f[l][:, :], func=AF.Copy), reads=[("ret_f", l)],
         writes=["ret_bf"])
    P.op("act", lambda e: e.activation(out=ssd_bf[:, :], in_=ssd_f[l][:, :], func=AF.Copy), reads=[("ssd_f", l)],
         writes=["ssd_bf"])

    P.mark("H")
    s = slot_of[(q, l, "A1")]
    prefetch_next(q, l, "A1")
    wA = V3(slots[s][:, 0:8 * 2048], 8)
    cos_q = V3(cossin[:, 0:256], 4)
    sin_q = V3(cossin[:, 256:512], 4)
    gam = [1.0 - 2.0 ** (-5 - h) for h in range(4)]
    cdec = [gm ** 128 for gm in gam]
    ret4 = V3(ret_f[l][:, :], 4)
    retb4 = V3(ret_bf[:, :], 4)
    qk4 = V4(ps[:, 0:1024], 8, 2)
    ta = V4(W[0][:, :], 8, 2)
    tb = V4(W[1][:, :], 8, 2)
    rot = V4(W[2][:, 0:512].bitcast(BF16), 8, 2)
    rot3 = V3(W[2][:, 0:512].bitcast(BF16), 8)
    qkT = W[2][:, 512:1024].bitcast(BF16)
    qT = V3(qkT[:, 0:512], 4)
    kT = V3(qkT[:, 512:1024], 4)
    vv = W[3][:, 0:512].bitcast(BF16)
    vb = V3(vv[:, 0:512], 4)
    vd = V3(vv[:, 512:1024], 4)
    sT = V3(W[3][:, 512:768].bitcast(BF16), 4)
    osb = W[4][:, 0:512]
    sqj = W[4][:, 512:768].bitcast(BF16)
    tg = W[5][:, 0:256].bitcast(BF16)
    sgs = [W[5][:, 256:512].bitcast(BF16), W[5][:, 512:768].bitcast(BF16)]
    sgk = [wk(5, 256, 512), wk(5, 512, 768)]
    yret = W[5][:, 768:1024].bitcast(BF16)
    ssum, ssqh, mean, var, rs4, nmr = (small[:, 16:20], small[:, 20:24], small[:, 24:28], small[:, 28:32],
                                       small[:, 32:36], small[:, 36:40])

    def a1_piece(t, n):
        tok = slice(t * 128, (t + 1) * 128)

        def proj(e):
            for kk in range(8):
                ins = e.matmul(bank(n), lhsT=hT3[:, kk, tok], rhs=wA[:, kk, n * 512:(n + 1) * 512], start=(kk == 0),
                               stop=(kk == 7))
            return ins
        P.op("pe", proj, reads=HTK + SK(s), writes=PK(n))

    def a1_E(t):
        sg = sgs[t % 2]
        P.op("act", lambda e: e.activation(out=tg, in_=bank(3), func=AF.Tanh, scale=0.5), reads=PK(3), writes=[*wk(5,0,256)])
        P.op("act", lambda e: e.activation(out=vv[:, 0:512], in_=bank(2), func=AF.Copy), reads=PK(2), writes=[*wk(3,0,256)])
        cb_ = cos_q[:, t, :].unsqueeze(1).unsqueeze(1).to_broadcast([128, 8, 2, 64])
        sb_ = sin_q[:, t, :].unsqueeze(1).unsqueeze(1).to_broadcast([128, 8, 2, 64])
        P.op("dve", lambda e: e.tensor_tensor(out=ta, in0=qk4, in1=cb_, op=ALU.mult), reads=PK(0, 1) + ["cossin"], writes=[*wk(0)])
        P.op("dve", lambda e: e.tensor_tensor(out=tb, in0=qk4, in1=sb_, op=ALU.mult), reads=PK(0, 1) + ["cossin"], writes=[*wk(1)])
        P.op("dve", lambda e: e.tensor_tensor(out=rot[:, :, 0, :], in0=ta[:, :, 0, :], in1=tb[:, :, 1, :], op=ALU.subtract),
             reads=[*wk(0), *wk(1)], writes=[*wk(2,0,512)])
        P.op("dve", lambda e: e.tensor_tensor(out=rot[:, :, 1, :], in0=tb[:, :, 0, :], in1=ta[:, :, 1, :], op=ALU.add),
             reads=[*wk(0), *wk(1)], writes=[*wk(2,0,512)])

        def trqk(e):
            for j in range(8):
                ins = e.transpose(out=bankb(4)[:, j * 128:(j + 1) * 128], in_=rot3[:, j, :], identity=identb)
            return ins
        P.op("pe", trqk, reads=[*wk(2,0,512), "cb16"], writes=PK(4))
        P.op("dve", lambda e: e.tensor_tensor(out=vd, in0=V3(bank(2), 4), in1=kdec.unsqueeze(2).to_broadcast([128, 4, 128]),
                                              op=ALU.mult), reads=PK(2) + ["cf"], writes=[*wk(3,256,512)])
        P.op("dve", lambda e: e.scalar_tensor_tensor(out=sg, in0=tg, scalar=1.0, in1=bank(3), op0=ALU.add, op1=ALU.mult),
             reads=[*wk(5,0,256)] + PK(3), writes=sgk[t % 2])
        P.op("dve", lambda e: e.tensor_tensor(out=qkT[:, 0:512], in0=bankb(4)[:, 0:512], in1=qdecT, op=ALU.mult),
             reads=PK(4) + ["cf"], writes=[*wk(2,512,768)])
        P.op("act", lambda e: e.activation(out=qkT[:, 512:1024], in_=bankb(4)[:, 512:1024], func=AF.Copy),
             reads=PK(4), writes=[*wk(2,768,1024)])

    def a1_scores(t):
        def scores(e):
            for h in range(4):
                e.matmul(bank(5)[:, h * 128:(h + 1) * 128], lhsT=kT[:, h, :], rhs=qT[:, h, :], start=True, stop=True)
            for h in range(4):
                ins = e.matmul(bank(6)[:, h * 128:(h + 1) * 128], lhsT=rot3[:, 4 + h, :], rhs=vd[:, h, :], start=True, stop=True)
            return ins
        P.op("pe", scores, reads=[*wk(2,512,768), *wk(2,768,1024), *wk(2,0,512), *wk(3,256,512)], writes=PK(5, 6))
        P.op("dve", lambda e: e.tensor_tensor(out=W[3][:, 512:768].bitcast(BF16), in0=bank(5), in1=maskT, op=ALU.mult),
             reads=PK(5) + ["cf"], writes=[*wk(3,512,768)])

    def a1_o(t):
        def omm(e):
            for h in range(4):
                e.matmul(bank(7)[:, h * 128:(h + 1) * 128], lhsT=sT[:, h, :], rhs=vb[:, h, :], start=True, stop=False)
                ins = e.matmul(bank(7)[:, h * 128:(h + 1) * 128], lhsT=qT[:, h, :], rhs=retb4[:, h, :], start=False, stop=True)
            return ins
        P.op("pe", omm, reads=[*wk(3,512,768), *wk(3,0,256), *wk(2,512,768), "ret_bf"], writes=PK(7))
        P.op("act", lambda e: e.activation(out=osb, in_=bank(7), func=AF.Copy), reads=PK(7), writes=[*wk(4,0,512)])
        for h in range(4):
            P.op("dve", lambda e, h=h: e.scalar_tensor_tensor(out=ret4[:, h, :], in0=ret4[:, h, :], scalar=float(cdec[h]),
                                                              in1=bank(6)[:, h * 128:(h + 1) * 128], op0=ALU.mult, op1=ALU.add),
                 reads=[("ret_f", l)] + PK(6), writes=[("ret_f", l)])
        P.op("act", lambda e: e.activation(out=ret_bf[:, :], in_=ret_f[l][:, :], func=AF.Copy), reads=[("ret_f", l)],
             writes=["ret_bf"])

    def a1_Ta(t):
        P.op("dve", lambda e: e.tensor_reduce(out=ssum, in_=V3(osb, 4), axis=AX.X, op=ALU.add), reads=[*wk(4,0,512)], writes=["ssum"])
        P.op("act", lambda e: e.activation(out=W[1][:, 0:512], in_=osb, func=AF.Square), reads=[*wk(4,0,512)], writes=[*wk(1,0,512)])
        P.op("dve", lambda e: e.tensor_reduce(out=ssqh, in_=V3(W[1][:, 0:512], 4), axis=AX.X, op=ALU.add), reads=[*wk(1,0,512)], writes=["ssqh"])
        P.op("dve", lambda e: e.tensor_scalar(out=mean, in0=ssum, scalar1=1.0 / 128, scalar2=None, op0=ALU.mult),
             reads=["ssum"], writes=["mean"])
        P.op("dve", lambda e: e.tensor_tensor(out=var, in0=mean, in1=mean, op=ALU.mult), reads=["mean"], writes=["var"])
        P.op("dve", lambda e: e.scalar_tensor_tensor(out=var, in0=ssqh, scalar=1.0 / 128, in1=var, op0=ALU.mult,
                                                     op1=ALU.subtract), reads=["ssqh", "var"], writes=["var"])
        P.op("dve", lambda e: e.tensor_scalar(out=var, in0=var, scalar1=EPS, scalar2=None, op0=ALU.add), reads=["var"],
             writes=["var"])
        rsqrt_small(rs4, var, 4, ["var"], ["rs4"])

    def a1_Tb(t):
        tok = slice(t * 128, (t + 1) * 128)
        sg = sgs[t % 2]
        P.op("dve", lambda e: e.tensor_scalar(out=rs4, in0=rs4, scalar1=0.5, scalar2=None, op0=ALU.mult), reads=["rs4"],
             writes=["rs4"])
        P.op("dve", lambda e: e.scalar_tensor_tensor(out=nmr, in0=mean, scalar=-1.0, in1=rs4, op0=ALU.mult, op1=ALU.mult),
             reads=["mean", "rs4"], writes=["nmr"])
        for h in range(4):
            P.op("act", lambda e, h=h: e.activation(out=osb[:, h * 128:(h + 1) * 128], in_=osb[:, h * 128:(h + 1) * 128],
                                                     func=AF.Identity, scale=rs4[:, h:h + 1], bias=nmr[:, h:h + 1]),
                 reads=[*wk(4,0,512), "rs4", "nmr"], writes=[*wk(4,0,512)])
        P.op("dve", lambda e: e.tensor_tensor(out=yret, in0=osb, in1=sg, op=ALU.mult), reads=[*wk(4,0,512)] + sgk[t % 2],
             writes=[*wk(5,768,1024)])

    def a1_Tc(t):
        tok = slice(t * 128, (t + 1) * 128)

        def tryr(e):
            for j in range(4):
                ins = e.transpose(out=bankb(4)[:, j * 128:(j + 1) * 128], in_=yret[:, j * 128:(j + 1) * 128], identity=identb)
            return ins
        P.op("pe", tryr, reads=[*wk(5,768,1024), "cb16"], writes=PK(4))
        P.op("act", lambda e: e.activation(out=yT3[:, 0:4, tok], in_=V3(bankb(4)[:, 0:512], 4), func=AF.Copy),
             reads=PK(4), writes=["ytemp"])
        if D3 and t == 0:
            dump("yret", yret, 512, BF16, reads=[*wk(5,768,1024)])

    for n in range(4):
        a1_piece(0, n)
    for t in range(NCQ):
        nxt = t + 1 < NCQ
        a1_E(t)
        if nxt:
            a1_piece(t + 1, 0)
        if t >= 1:
            a1_Ta(t - 1)
        a1_scores(t)
        if nxt:
            a1_piece(t + 1, 1)
        if t >= 1:
            a1_Tb(t - 1)
            a1_Tc(t - 1)
        a1_o(t)
        if nxt:
            a1_piece(t + 1, 2)
            a1_piece(t + 1, 3)
            if t + 1 == NCQ - 1:
                prefetch_next(q, l, "A2", early=True)

    def a1_last():
        a1_Ta(NCQ - 1)
        a1_Tb(NCQ - 1)

    def merge_pass(grp, nkb, first, last, after_tail=None, inject=None, inject_early=None):
        s2 = slot_of[(q, l, grp)]
        prefetch_next(q, l, grp)
        wg = V3(slots[s2][:, 0:8192], 8)
        wb = V3(slots[s2][:, 8192:8192 + nkb * 1024], nkb)
        wo = V3(slots[s2][:, 12288:12288 + 8192], 8)
        unit = [0]

        def tail(t):
            def trm(e):
                for j in range(8):
                    ins = e.transpose(out=bankb(4)[:, j * 128:(j + 1) * 128], in_=mg3[:, t, j * 128:(j + 1) * 128],
                                      identity=identb)
                return ins
            P.op("pe", trm, reads=[("mg", t, 0), ("mg", t, 1), "cb16"], writes=PK(4))
            mT = V3(W[2][:, 0:512].bitcast(BF16), 8)
            P.op("act", lambda e: e.activation(out=W[2][:, 0:512].bitcast(BF16), in_=bankb(4), func=AF.Copy),
                 reads=PK(4), writes=[*wk(2, 0, 512)])

            def om(e):
                for n in range(2):
                    for kk in range(8):
                        ins = e.matmul(bank(5 + n), lhsT=mT[:, kk, :], rhs=wo[:, kk, n * 512:(n + 1) * 512],
                                       start=(kk == 0), stop=(kk == 7))
                return ins
            P.op("pe", om, reads=[*wk(2, 0, 512)] + SK(s2), writes=PK(5, 6))
            tmp2 = W[3][:, :]
            P.op("dve", lambda e: e.tensor_tensor(out=tmp2, in0=ps[:, 2560:3584], in1=rg_bc[l][:, :], op=ALU.mult),
                 reads=PK(5, 6) + [("rg_bc", l)], writes=[*wk(3)])
            P.op("dve", lambda e: e.tensor_tensor(out=xq3[:, t, :], in0=xq3[:, t, :], in1=tmp2, op=ALU.add),
                 reads=[*wk(3), ("xq", t)], writes=[("xq", t)])

        if inject_early is not None:
            inject_early()
        for t in range(NCQ):
            tok = slice(t * 128, (t + 1) * 128)
            if t == 3 and inject is not None:
                inject()
            for n in range(2):
                nrot = 2 if last else 4
                par = unit[0] % nrot
                unit[0] += 1
                gb, pbk = 2 * par, 2 * par + 1

                def gp(e, tok=tok, n=n, gb=gb, pbk=pbk):
                    for kk in range(8):
                        e.matmul(bank(gb), lhsT=hT3[:, kk, tok], rhs=wg[:, kk, n * 512:(n + 1) * 512], start=(kk == 0),
                                 stop=(kk == 7))
                    for kk in range(nkb):
                        ins = e.matmul(bank(pbk), lhsT=yT3[:, kk, tok], rhs=wb[:, kk, n * 512:(n + 1) * 512],
                                       start=(kk == 0), stop=(kk == nkb - 1))
                    return ins
                P.op("pe", gp, reads=HTK + ["ytemp"] + SK(s2), writes=PK(gb, pbk))
                tgate = W[0][:, par * 256:(par + 1) * 256].bitcast(BF16)
                tgk = wk(0, par * 256, par * 256 + 256)
                P.op("act", lambda e, tgate=tgate, gb=gb: e.activation(out=tgate, in_=bank(gb), func=AF.Tanh, scale=0.5),
                     reads=PK(gb), writes=tgk)
                mgs = mg3[:, t, n * 512:(n + 1) * 512]
                if first:
                    P.op("dve", lambda e, tgate=tgate, pbk=pbk, mgs=mgs: e.scalar_tensor_tensor(
                        out=mgs, in0=tgate, scalar=1.0, in1=bank(pbk), op0=ALU.add, op1=ALU.mult),
                        reads=tgk + PK(pbk), writes=[("mg", t, n)])
                else:
                    tmp = W[1 + par // 2][:, (par % 2) * 512:(par % 2 + 1) * 512]
                    tmk = wk(1 + par // 2, (par % 2) * 512, (par % 2) * 512 + 512)
                    P.op("dve", lambda e, tgate=tgate, pbk=pbk, tmp=tmp: e.scalar_tensor_tensor(
                        out=tmp, in0=tgate, scalar=1.0, in1=bank(pbk), op0=ALU.add, op1=ALU.mult),
                        reads=tgk + PK(pbk), writes=tmk)
                    P.op("dve", lambda e, mgs=mgs, tmp=tmp: e.tensor_tensor(out=mgs, in0=mgs, in1=tmp, op=ALU.add),
                         reads=tmk + [("mg", t, n)], writes=[("mg", t, n)])
            if last and t >= 1:
                tail(t - 1)
                if after_tail is not None:
                    after_tail(t - 1)
        if last:
            tail(NCQ - 1)
            if after_tail is not None:
                after_tail(NCQ - 1)

    P.mark("A1")
    merge_pass("A2", 4, True, False, inject_early=a1_last, inject=lambda: a1_Tc(NCQ - 1))
    P.mark("A2")
    if D3:
        dump("mgA", merged[:, 0:1024], 1024, BF16, reads=[("mg", 0, 0), ("mg", 0, 1)])

    P.dma("sp", lambda e: e.dma_start(out=gbc[:, :], in_=sng_d[l]), writes=["gbc"])
    s = slot_of[(q, l, "B1")]
    prefetch_next(q, l, "B1")
    wB = V3(slots[s][:, 0:8 * 2576], 8)
    def dtmm(e):
        for t in range(NCQ):
            for kk in range(8):
                ins = e.matmul(bank(0)[:, t * 16:(t + 1) * 16], lhsT=hT3[:, kk, t * 128:(t + 1) * 128], rhs=wB[:, kk, 2560:2576],
                               start=(kk == 0), stop=(kk == 7))
        return ins
    P.op("pe", dtmm, reads=HTK + SK(s), writes=PK(0))
    dt_all = small[:, 64:128]
    a_all = small[:, 128:192]
    eacs = small[:, 192:256]
    dstt = small[:, 256:320]
    cdq = small[:, 320:384]
    dth = small[:, 384:448]
    P.op("dve", lambda e: e.tensor_tensor(out=V3(dt_all, 4), in0=V3(bank(0)[:, 0:64], 4),
                                          in1=dtb[l][:, :].unsqueeze(1).to_broadcast([128, 4, 16]), op=ALU.add),
         reads=PK(0) + [("dtb", l)], writes=["dt_all"])
    P.op("dve", lambda e: e.tensor_scalar(out=dt_all, in0=dt_all, scalar1=30.0, scalar2=None, op0=ALU.min), reads=["dt_all"],
         writes=["dt_all"])
    P.op("act", lambda e: e.activation(out=dt_all, in_=dt_all, func=AF.Exp), reads=["dt_all"], writes=["dt_all"])
    P.op("act", lambda e: e.activation(out=dt_all, in_=dt_all, func=AF.Ln, bias=1.0), reads=["dt_all"], writes=["dt_all"])
    P.op("dve", lambda e: e.tensor_tensor(out=V3(a_all, 4), in0=V3(dt_all, 4),
                                          in1=A_bc[l][:, :].unsqueeze(1).to_broadcast([128, 4, 16]), op=ALU.mult),
         reads=["dt_all", ("A_bc", l)], writes=["a_all"])
    xp3 = V3(xpad[:, :], 12)
    xc3 = V3(xc[:, :], 12)
    xt3 = V3(xtail[l][:, :], 12)
    ssd16 = V3(ssd_f[l][:, :], 16)
    tx = W[0][:, 0:768].bitcast(BF16)
    eseg = W[0][:, :].bitcast(BF16)
    Ybf = W[1][:, :].bitcast(BF16)
    MTb = W[2][:, :].bitcast(BF16)
    MT = V3(MTb, 16)
    xdt = W[3][:, 0:512].bitcast(BF16)
    xdtd = W[3][:, 512:1024].bitcast(BF16)
    y1 = W[4][:, :]
    sz = W[5][:, 0:512].bitcast(BF16)
    yn = W[5][:, 512:1024].bitcast(BF16)
    btok = V3(misc1[:, 256:512], 2)
    Gm = V3(misc1[:, 0:256], 2)
    xsT = V3(bankb(2), 16)
    ssy, msy, rsy = small[:, 40:41], small[:, 41:42], small[:, 42:43]
    tzb = W[4][:, :]

    def b1_proj(t):
        tok = slice(t * 128, (t + 1) * 128)
        P.op("pool", lambda e: e.tensor_tensor(
            out=V3(Ybf, 16), in0=a_all[:, t * 16:(t + 1) * 16].unsqueeze(2).to_broadcast([128, 16, 128]),
            in1=Uf.unsqueeze(1).to_broadcast([128, 16, 128]), op=ALU.mult),
            reads=["a_all", "cf", *wk(4)], writes=[*wk(1)])
        P.op("act", lambda e: e.activation(out=xp3[:, :, 0:3], in_=xt3, func=AF.Copy), reads=[("xtail", l)], writes=["xpad_h"])

        def zp(e):
            for n in range(2):
                for kk in range(8):
                    ins = e.matmul(bank(n), lhsT=hT3[:, kk, tok], rhs=wB[:, kk, n * 512:(n + 1) * 512], start=(kk == 0),
                                   stop=(kk == 7))
            return ins
        P.op("pe", zp, reads=HTK + SK(s), writes=PK(0, 1))
        for j in range(3):
            def xp(e, j=j):
                for m in range(4 * j, 4 * j + 4):
                    for kk in range(8):
                        ins = e.matmul(ps[:, 1024 + m * 128:1024 + (m + 1) * 128],
                                       lhsT=wB[:, kk, 1024 + m * 128:1024 + (m + 1) * 128],
                                       rhs=hT3[:, kk, tok], start=(kk == 0), stop=(kk == 7))
                return ins
            P.op("pe", xp, reads=HTK + SK(s), writes=PK(2 + j))
            P.op("act", lambda e, j=j: e.activation(out=xp3[:, 4 * j:4 * j + 4, 3:131], in_=V3(bank(2 + j), 4), func=AF.Copy),
                 reads=PK(2 + j), writes=[("xpad_b", j)])
        P.op("act", lambda e: e.activation(out=xt3, in_=xp3[:, :, 128:131], func=AF.Copy),
             reads=[("xpad_b", j) for j in range(3)], writes=[("xtail", l)])

    def b1_front(t):
        P.op("act", lambda e: e.activation(out=sz, in_=ps[:, 0:1024], func=AF.Tanh, scale=0.5), reads=PK(0, 1), writes=[*wk(5,0,512)])
        P.op("dve", lambda e: e.scalar_tensor_tensor(out=sz, in0=sz, scalar=1.0, in1=ps[:, 0:1024], op0=ALU.add, op1=ALU.mult),
             reads=[*wk(5,0,512)] + PK(0, 1), writes=[*wk(5,0,512)])
        for j in range(3):
            def conv(e, j=j):
                for m in range(4 * j, 4 * j + 4):
                    for k in range(4):
                        e.matmul(ps[:, 2560 + m * 128:2560 + (m + 1) * 128], lhsT=dg4[:, m, k, :], rhs=xp3[:, m, k:k + 128],
                                 start=(k == 0), stop=False)
                    ins = e.matmul(ps[:, 2560 + m * 128:2560 + (m + 1) * 128], lhsT=cbrow[0:1, m * 128:(m + 1) * 128],
                                   rhs=onesrow[0:1, :], start=False, stop=True)
                return ins
            P.op("pe", conv, reads=["xpad_h", ("xpad_b", j), "diagw", "cbrow", "onesrow"], writes=PK(5 + j))
            txj = tx[:, j * 512:(j + 1) * 512]
            txk = wk(0, j * 256, j * 256 + 256)
            P.op("act", lambda e, j=j, txj=txj: e.activation(out=txj, in_=bank(5 + j), func=AF.Tanh, scale=0.5), reads=PK(5 + j),
                 writes=txk)
            P.op("dve", lambda e, j=j, txj=txj: e.scalar_tensor_tensor(out=xc[:, j * 512:(j + 1) * 512], in0=txj, scalar=1.0,
                                                                       in1=bank(5 + j), op0=ALU.add, op1=ALU.mult),
                 reads=txk + PK(5 + j), writes=[("xc", j)])
        if t >= 1:
            b1_gate_pe(t - 1)

        def trx(e):
            for m in range(8):
                ins = e.transpose(out=bankb(2)[:, m * 128:(m + 1) * 128], in_=xc3[:, m, :], identity=identb)
            return ins
        P.op("pe", trx, reads=[("xc", 0), ("xc", 1), "cb16"], writes=PK(2))

        def trb(e):
            for gI in range(2):
                e.transpose(out=bankb(3)[:, gI * 128:(gI + 1) * 128], in_=xc3[:, 8 + gI, :], identity=identb)
            for gI in range(2):
                ins = e.matmul(ps[:, 1792 + gI * 128:1792 + (gI + 1) * 128], lhsT=xc3[:, 8 + gI, :], rhs=xc3[:, 10 + gI, :],
                               start=True, stop=True)
            return ins
        P.op("pe", trb, reads=[("xc", 2), "cb16"], writes=PK(3))

        def segmm(e):
            for n in range(4):
                ins = e.matmul(bank(4 + n), lhsT=xgt, rhs=Ybf[:, n * 512:(n + 1) * 512], start=True, stop=True)
            return ins
        P.op("pe", segmm, reads=[*wk(1), "cb16"], writes=PK(4, 5, 6, 7))
        P.op("dve", lambda e: e.tensor_tensor(out=Gm, in0=V3(ps[:, 1792:2048], 2),
                                              in1=cmask.unsqueeze(1).to_broadcast([128, 2, 128]), op=ALU.mult),
             reads=PK(3) + ["cf"], writes=["Gm"])
        P.op("dve", lambda e: e.tensor_tensor(out=V3(xdt, 16), in0=xsT,
                                              in1=dth[:, t * 16:(t + 1) * 16].unsqueeze(2).to_broadcast([128, 16, 64]),
                                              op=ALU.mult), reads=PK(2) + ["dth"], writes=[*wk(3,0,512)])
        for hg in range(2):
            P.op("act", lambda e, hg=hg: e.activation(out=eseg[:, hg * 1024:(hg + 1) * 1024], in_=ps[:, 2048 + hg * 1024:3072 + hg * 1024],
                                                       func=AF.Exp), reads=PK(4 + 2 * hg, 5 + 2 * hg), writes=wk(0, hg * 512, hg * 512 + 512))
            P.op("dve", lambda e, hg=hg: e.tensor_tensor(out=V3(MTb[:, hg * 1024:(hg + 1) * 1024], 8),
                                                         in0=V3(eseg[:, hg * 1024:(hg + 1) * 1024], 8),
                                                         in1=Gm[:, hg:hg + 1, :].to_broadcast([128, 8, 128]), op=ALU.mult),
                 reads=wk(0, hg * 512, hg * 512 + 512) + ["Gm"], writes=wk(2, hg * 512, hg * 512 + 512))
        P.op("dve", lambda e: e.tensor_tensor(out=V3(xdtd, 16), in0=xsT,
                                              in1=dstt[:, t * 16:(t + 1) * 16].unsqueeze(2).to_broadcast([128, 16, 64]),
                                              op=ALU.mult), reads=PK(2) + ["dstt"], writes=[*wk(3,512,1024)])
        P.op("act", lambda e: e.activation(out=misc1[:, 256:512], in_=bankb(3)[:, 0:256], func=AF.Copy), reads=PK(3),
             writes=["btok"])
        if D3 and t == 0:
            dump("xc", xc[:, :], 1536, BF16, reads=[("xc", 0), ("xc", 1), ("xc", 2)])

    def b1_scan(t):
        def ydiag(e, bk):
            if True:
                for h in range(bk * 8, bk * 8 + 8):
                    e.matmul(ps[:, 2048 + h * 64:2048 + (h + 1) * 64], lhsT=MT[:, h, :], rhs=xdt[:, h * 64:(h + 1) * 64],
                             start=(h == bk * 8), stop=False, skip_group_check=True)
                for m in range(bk * 4, bk * 4 + 4):
                    ins = e.matmul(ps[:, 2048 + m * 128:2048 + (m + 1) * 128], lhsT=xc3[:, m, :], rhs=dm3[:, m, :],
                                   start=False, stop=True, skip_group_check=True)
            return ins
        for bk in range(2):
            P.op("pe", lambda e, bk=bk: ydiag(e, bk), reads=wk(2, bk * 512, bk * 512 + 512) + [*wk(3,0,512), ("xc", 0), ("xc", 1), "dmt"],
                 writes=PK(4 + bk))

        def yoff(e):
            for gI in range(2):
                ins = e.matmul(bank(6 + gI), lhsT=xc3[:, 10 + gI, :], rhs=ssd_bf[:, gI * 512:(gI + 1) * 512], start=True, stop=True)
            return ins
        P.op("pe", yoff, reads=[("xc", 2), "ssd_bf"], writes=PK(6, 7))

        def smm(e):
            for gI in range(2):
                ins = e.matmul(ps[:, 1024 + gI * 512:1024 + (gI + 1) * 512], lhsT=btok[:, gI, :],
                               rhs=xdtd[:, gI * 512:(gI + 1) * 512], start=True, stop=True)
            return ins
        P.op("pe", smm, reads=["btok", *wk(3,512,1024)], writes=PK(2, 3))
        P.op("dve", lambda e: e.tensor_tensor(out=V3(y1, 16), in0=V3(ps[:, 3072:4096], 16),
                                              in1=eacs[:, t * 16:(t + 1) * 16].unsqueeze(2).to_broadcast([128, 16, 64]),
                                              op=ALU.mult), reads=PK(6, 7) + ["eacs"], writes=[*wk(4)])
        P.op("dve", lambda e: e.tensor_tensor(out=y1, in0=y1, in1=ps[:, 2048:3072], op=ALU.add), reads=[*wk(4)] + PK(4, 5),
             writes=[*wk(4)])
        P.op("dve", lambda e: e.tensor_tensor(out=ssd16, in0=ssd16,
                                               in1=cdq[:, t * 16:(t + 1) * 16].unsqueeze(2).to_broadcast([128, 16, 64]),
                                               op=ALU.mult), reads=[("ssd_f", l), "cdq"], writes=[("ssd_f", l)])
        P.op("dve", lambda e: e.tensor_tensor(out=ssd_f[l][:, :], in0=ssd_f[l][:, :], in1=ps[:, 1024:2048], op=ALU.add),
             reads=[("ssd_f", l)] + PK(2, 3), writes=[("ssd_f", l)])
        P.op("act", lambda e: e.activation(out=ssd_bf[:, :], in_=ssd_f[l][:, :], func=AF.Copy), reads=[("ssd_f", l)],
             writes=["ssd_bf"])
        if D3 and t == 0:
            dump("y1a", y1, 1024, reads=[*wk(4)])

    def b1_gate(t):
        P.op("dve", lambda e: e.tensor_tensor(out=y1, in0=y1, in1=sz, op=ALU.mult), reads=[*wk(4), *wk(5,0,512)], writes=[*wk(4)])
        P.op("act", lambda e: e.activation(out=yn, in_=y1, func=AF.Square, accum_out=ssy), reads=[*wk(4)], writes=[*wk(5,512,1024), "ssy"])
        P.op("dve", lambda e: e.tensor_scalar(out=msy, in0=ssy, scalar1=1.0 / 1024, scalar2=4.0 * EPS, op0=ALU.mult,
                                              op1=ALU.add), reads=["ssy"], writes=["msy"])
        rsqrt_small(rsy, msy, 1, ["msy"], ["rsy"])
        P.op("dve", lambda e: e.scalar_tensor_tensor(out=yn, in0=y1, scalar=rsy, in1=gbc[:, :], op0=ALU.mult, op1=ALU.mult),
             reads=[*wk(4), "rsy", "gbc"], writes=[*wk(5,512,1024)])
        if D3 and t == 0:
            dump("y1", y1, 1024, reads=[*wk(4)])
            dump("yn", yn, 1024, BF16, reads=[*wk(5,512,1024)])

    def b1_gate_pe(t, bk=0):
        tok = slice(t * 128, (t + 1) * 128)

        def tryn(e):
            for m in range(8):
                ins = e.transpose(out=bankb(bk)[:, m * 128:(m + 1) * 128], in_=yn[:, m * 128:(m + 1) * 128], identity=identb)
            return ins
        P.op("pe", tryn, reads=[*wk(5,512,1024), "cb16"], writes=PK(bk))
        P.op("act", lambda e: e.activation(out=yT3[:, :, tok], in_=V3(bankb(bk), 8), func=AF.Copy), reads=PK(bk),
             writes=["ytemp"])

    def dt_b():
        P.op("pe", lambda e: e.matmul(bank(7)[:, 0:64], lhsT=Uf, rhs=a_all, start=True, stop=True), reads=["a_all", "cf"],
             writes=PK(7))
        P.op("pe", lambda e: e.matmul(bank(7)[:, 64:128], lhsT=onesf, rhs=a_all, start=True, stop=True), reads=["a_all", "cf"],
             writes=PK(7))
        P.op("act", lambda e: e.activation(out=eacs, in_=bank(7)[:, 0:64], func=AF.Exp, bias=math.log(0.25)), reads=PK(7),
             writes=["eacs"])
        P.op("act", lambda e: e.activation(out=cdq, in_=bank(7)[:, 64:128], func=AF.Exp), reads=PK(7), writes=["cdq"])
        acs_sb = small[:, 448:512]
        P.op("act", lambda e: e.activation(out=acs_sb, in_=bank(7)[:, 0:64], func=AF.Copy), reads=PK(7), writes=["acs_sb"])
        P.op("dve", lambda e: e.tensor_tensor(out=dstt, in0=bank(7)[:, 64:128], in1=acs_sb, op=ALU.subtract),
             reads=PK(7) + ["acs_sb"], writes=["dstt"])
        P.op("act", lambda e: e.activation(out=dstt, in_=dstt, func=AF.Exp), reads=["dstt"], writes=["dstt"])
        P.op("dve", lambda e: e.tensor_scalar(out=dth, in0=dt_all, scalar1=0.5, scalar2=None, op0=ALU.mult), reads=["dt_all"],
             writes=["dth"])
        P.op("dve", lambda e: e.tensor_tensor(out=dstt, in0=dstt, in1=dth, op=ALU.mult), reads=["dstt", "dth"], writes=["dstt"])
        if D3:
            dump("small", small[:, :], 512, reads=["dt_all", "a_all", "eacs", "dstt", "cdq", "dth"])


    for t in range(NCQ):
        b1_proj(t)
        if t == NCQ - 1:
            prefetch_next(q, l, "B2", early=True)
        if t == 0:
            dt_b()
        if t >= 1:
            b1_gate(t - 1)
        b1_front(t)
        b1_scan(t)

    def b1_last():
        b1_gate(NCQ - 1)

    P.mark("B1")
    merge_pass("B2", 8, False, False, inject_early=b1_last, inject=lambda: b1_gate_pe(NCQ - 1, bk=7))
    P.mark("B2")

    ds4 = V4(diagsc[:, 0:12 * 128], 4, 3)
    P.op("pool", lambda e: e.tensor_tensor(out=V3(diagsc[:, 0:12 * 128], 12), in0=identb.unsqueeze(1).to_broadcast([128, 12, 128]),
                                           in1=scw[l][:, :].unsqueeze(2).to_broadcast([128, 12, 128]), op=ALU.mult),
         reads=["cb16", ("scw", l)], writes=["diagw"])

    if l == n_layers_ - 1 and q + 1 < n_quarters_:
        emit_rotary(q + 1)
    s = slot_of[(q, l, "C1")]
    prefetch_next(q, l, "C1")
    wC = V3(slots[s][:, 0:8 * 2048], 8)
    up3 = V3(xpad[:, 0:4 * 130], 4)
    ut3 = V3(utail[l][:, :], 4)
    def c1_proj(t):
        tok = slice(t * 128, (t + 1) * 128)
        b0 = 4 * (t % 2)

        def cp(e):
            for m in range(16):
                for kk in range(8):
                    ins = e.matmul(ps[:, b0 * 512 + m * 128:b0 * 512 + (m + 1) * 128], lhsT=wC[:, kk, m * 128:(m + 1) * 128],
                                   rhs=hT3[:, kk, tok], start=(kk == 0), stop=(kk == 7))
            return ins
        P.op("pe", cp, reads=HTK + SK(s), writes=PK(b0, b0 + 1, b0 + 2, b0 + 3))

    c1_proj(0)
    for t in range(NCQ):
        tok = slice(t * 128, (t + 1) * 128)
        par = t % 2
        b0 = 4 * par
        if t + 1 < NCQ:
            c1_proj(t + 1)
            if t + 1 == NCQ - 1:
                prefetch_next(q, l, "C2", early=True)
        chs = W[0][:, par * 512:par * 512 + 256].bitcast(BF16)
        chk = wk(0, par * 512, par * 512 + 256)
        tcg = W[0][:, par * 512 + 256:par * 512 + 512].bitcast(BF16)
        tck = wk(0, par * 512 + 256, par * 512 + 512)
        s1 = W[1][:, par * 512:(par + 1) * 512]
        s1k = wk(1, par * 512, par * 512 + 512)
        P.op("act", lambda e, chs=chs, b0=b0: e.activation(out=chs, in_=bank(b0 + 2), func=AF.Copy), reads=PK(b0 + 2), writes=chk)
        P.op("act", lambda e, tcg=tcg, b0=b0: e.activation(out=tcg, in_=bank(b0 + 3), func=AF.Tanh, scale=0.5), reads=PK(b0 + 3),
             writes=tck)
        P.op("act", lambda e: e.activation(out=up3[:, :, 0:2], in_=ut3, func=AF.Copy), reads=[("utail", l)], writes=["xpad_h", ("xpad_b", 0), ("xpad_b", 1), ("xpad_b", 2)])
        P.op("dve", lambda e, chs=chs, b0=b0: e.tensor_tensor(out=up3[:, :, 2:130], in0=V3(bank(b0 + 1), 4), in1=V3(chs, 4), op=ALU.mult),
             reads=PK(b0 + 1) + chk, writes=["xpad_h", ("xpad_b", 0), ("xpad_b", 1), ("xpad_b", 2)])
        P.op("act", lambda e: e.activation(out=ut3, in_=up3[:, :, 128:130], func=AF.Copy), reads=["xpad_h", ("xpad_b", 0), ("xpad_b", 1), ("xpad_b", 2)],
             writes=[("utail", l)])

        def cconv(e, b0=b0):
            for m in range(4):
                for k in range(3):
                    ins = e.matmul(bank(b0 + 1)[:, m * 128:(m + 1) * 128], lhsT=ds4[:, m, k, :], rhs=up3[:, m, k:k + 128],
                                   start=(k == 0), stop=(k == 2))
            return ins
        P.op("pe", cconv, reads=["xpad_h", ("xpad_b", 0), ("xpad_b", 1), ("xpad_b", 2), "diagw"], writes=PK(b0 + 1))
        P.op("dve", lambda e, s1=s1, tcg=tcg, b0=b0: e.scalar_tensor_tensor(out=s1, in0=tcg, scalar=1.0, in1=bank(b0 + 3), op0=ALU.add,
                                                                            op1=ALU.mult), reads=tck + PK(b0 + 3), writes=s1k)
        P.op("dve", lambda e, s1=s1, b0=b0: e.tensor_tensor(out=s1, in0=s1, in1=bank(b0), op=ALU.mult), reads=s1k + PK(b0), writes=s1k)
        P.op("dve", lambda e, tok=tok, s1=s1, b0=b0: e.tensor_tensor(out=yT3[:, 0:4, tok], in0=V3(s1, 4), in1=V3(bank(b0 + 1), 4),
                                                                     op=ALU.mult), reads=s1k + PK(b0 + 1), writes=["ytemp"])

    P.mark("C1")
    if l + 1 < n_layers_:
        emit_prep(l + 1)
    elif q + 1 < n_quarters_:
        emit_prep(0)
    is_last_layer = (l == g["n_layers"] - 1)
    if is_last_layer:
        P.dma("sp", lambda e: e.dma_start(out=gbc[:, :], in_=g["fg_d"]), writes=["gbc"])
    merge_pass("C2", 4, False, True, after_tail=(g["final_cb"] if is_last_layer else None))
    if D3:
        dump("mgC", merged[:, 0:1024], 1024, BF16, reads=[("mg", 0, 0), ("mg", 0, 1)])
        dump("xq0", xq[:, 0:1024], 1024, reads=[("xq", 0)])


def _consts():
    cf = np.zeros((128, C_TOT), np.float32)
    idx = np.arange(128)
    cf[:, C_U:C_U + 128] = (idx[:, None] <= idx[None, :])
    cf[:, C_ONES:C_ONES + 128] = 1.0
    cf[:, C_CMASK:C_CMASK + 128] = 0.25 * (idx[None, :] >= idx[:, None])
    gam = 1.0 - np.exp2(-5.0 - np.arange(4))
    for h in range(4):
        lg = math.log(gam[h])
        m = np.where(idx[None, :] >= idx[:, None], np.exp(-(idx[:, None] + 1.0) * lg), 0.0)
        cf[:, C_MASKT + h * 128:C_MASKT + (h + 1) * 128] = m
        cf[:, C_QDEC + h * 128:C_QDEC + (h + 1) * 128] = (np.exp((idx + 1.0) * lg) * 128 ** -0.5)[None, :]
        cf[:, C_KDEC + h] = np.exp((127.0 - idx) * lg)
    cf[:, C_INVF:C_INVF + 64] = (10000.0 ** (-np.arange(0, 128, 2, dtype=np.float64) / 128))[None, :]
    cf[:, C_NHALF:C_NHALF + 16] = -0.5
    cf[:, C_TWOPI:C_TWOPI + 1] = 2.0 * math.pi
    cb = np.zeros((128, 256), np.float32)
    cb[:, 0:128] = np.eye(128)
    cb[:, 128:256] = (idx[:, None] > idx[None, :])
    return cf, cb.astype(ml_dtypes.bfloat16)


_PROGRAM = {}


def _get_program(dbg=False, n_quarters=NQ, n_layers=DEPTH):
    key = (dbg, n_quarters, n_layers)
    if key not in _PROGRAM:
        _PROGRAM[key] = build_program(dbg, n_quarters, n_layers)
    return _PROGRAM[key]


def make_in_maps(x, c, positions, norm_g, w_ada, b_ada, w_in, ssm_conv_w, ssm_conv_b, ssm_dt_bias, ssm_a_log, ssm_d,
                 ssm_norm_g, sc_conv_w, w_br_ret, w_br_ssm, w_br_sc, w_out, final_norm_g):
    f = lambda a: np.ascontiguousarray(np.asarray(a, dtype=np.float32))
    cf, cb = _consts()
    B = x.shape[0]
    rep = lambda a: np.ascontiguousarray(np.broadcast_to(np.asarray(a, np.float32)[:, None, :], (DEPTH, 128, a.shape[-1])))
    shared = {
        "w_in": f(w_in), "w_ada": f(w_ada), "w_br_ret": f(w_br_ret), "w_br_ssm": f(w_br_ssm), "w_br_sc": f(w_br_sc),
        "w_out": f(w_out),
        "norm_g_fm": f(np.asarray(norm_g).reshape(DEPTH, 8, 128).transpose(0, 2, 1)),
        "b_ada_bc": rep(np.asarray(b_ada)),
        "conv_w_fm": f(np.asarray(ssm_conv_w).reshape(DEPTH, 4, 12, 128).transpose(0, 3, 2, 1).reshape(DEPTH, 128, 48)),
        "conv_b_row": f(np.asarray(ssm_conv_b).reshape(DEPTH, 1, 1536)),
        "dt_bias_bc": rep(np.asarray(ssm_dt_bias)),
        "a_log_bc": rep(np.asarray(ssm_a_log)),
        "d_fm": f(np.repeat(np.asarray(ssm_d), 64, axis=1).reshape(DEPTH, 8, 128).transpose(0, 2, 1)),
        "ssm_norm_g_bc": rep(np.asarray(ssm_norm_g)),
        "sc_w_fm": f(np.asarray(sc_conv_w).reshape(DEPTH, 3, 4, 128).transpose(0, 3, 2, 1).reshape(DEPTH, 128, 12)),
        "final_g_bc": np.ascontiguousarray(np.broadcast_to(np.asarray(final_norm_g, np.float32)[None, :], (128, D))),
        "cf32": cf, "cbf16": cb,
    }
    maps = []
    for b in range(B):
        m = dict(shared)
        m["x"] = f(x[b])
        m["c"] = f(np.asarray(c[b]).reshape(8, 128).T)
        m["pos"] = np.ascontiguousarray(np.asarray(positions[b], np.int32).reshape(16, 128).T)
        maps.append(m)
    return maps


def kernel(**inputs):
    maps = make_in_maps(**inputs)
    nc = _get_program()
    res = run_bass_kernel_spmd(nc, maps, core_ids=list(range(len(maps))))
    return np.stack([np.asarray(r["out"], dtype=np.float32) for r in res.results], axis=0)
```

```python
import math
from contextlib import ExitStack

import numpy as np
import ml_dtypes
import concourse.bass as bass
import concourse.mybir as mybir
from concourse.bass_utils import run_bass_kernel_spmd

F32 = mybir.dt.float32
BF16 = mybir.dt.bfloat16
I32 = mybir.dt.int32
AF = mybir.ActivationFunctionType
ALU = mybir.AluOpType
AX = mybir.AxisListType

ENGS = ["pe", "act", "dve", "pool", "sp"]
N_DMA_SEMS = 36
DMA_SEM_POOLS = {"sp": list(range(0, 12)), "pool": list(range(12, 36))}

D = 1024
L = 2048
DEPTH = 2
NQ = 4
NCQ = 4
EPS = 1e-6
IN_W = 9744
O_Q, O_K, O_V, O_G, O_Z, O_XBC, O_DT, O_CB = 0, 512, 1024, 1536, 2048, 3072, 4608, 4624
O_G0, O_G1, O_G2 = 6672, 7696, 8720
NP_SLOT = 24
SLOT_COLS = 8 * 2576

C_U, C_ONES, C_CMASK, C_MASKT, C_QDEC, C_KDEC, C_INVF, C_NHALF, C_TWOPI = (
    0, 128, 256, 384, 896, 1408, 1412, 1476, 1492)
C_TOT = 1496


class Plan:
    def __init__(self):
        self.ops = {e: [] for e in ENGS}
        self.count = {e: 0 for e in ENGS}
        self.waited = {e: {} for e in ENGS}
        self.last_write = {}
        self.readers = {}
        self.dma_count = [0] * N_DMA_SEMS
        self.dma_next = {k: 0 for k in DMA_SEM_POOLS}
        self.enabled = True
        self.stop_at = None

    def mark(self, name):
        if self.stop_at is not None and name == self.stop_at:
            self.enabled = False

    def _deps(self, eng, reads, writes, extra=()):
        deps = {}

        def add(tok):
            s, v = tok
            if deps.get(s, 0) < v:
                deps[s] = v

        for k in reads:
            if k in self.last_write:
                add(self.last_write[k])
        for k in writes:
            if k in self.last_write:
                add(self.last_write[k])
            for s, v in self.readers.get(k, {}).items():
                add((s, v))
        for t in extra:
            add(t)
        waits = []
        for s, v in deps.items():
            if s == eng and eng == "pe":
                continue
            if self.waited[eng].get(s, 0) < v:
                waits.append((s, v))
                self.waited[eng][s] = v
        return waits

    def _record(self, tok, reads, writes):
        for k in writes:
            self.last_write[k] = tok
            self.readers[k] = {}
        for k in reads:
            d = self.readers.setdefault(k, {})
            if d.get(tok[0], 0) < tok[1]:
                d[tok[0]] = tok[1]

    @staticmethod
    def _excl(reads, writes):
        ps = [k for k in reads if isinstance(k, tuple) and k[0] == "ps"]
        if ps:
            reads = [k for k in reads if not (isinstance(k, tuple) and k[0] == "ps")]
            writes = list(writes) + [k for k in ps if k not in writes]
        return reads, writes

    def op(self, eng, fn, reads=(), writes=()):
        if not self.enabled:
            return None
        reads, writes = self._excl(reads, writes)
        waits = self._deps(eng, reads, writes)
        self.count[eng] += 1
        tok = (eng, self.count[eng])
        self.ops[eng].append((waits, fn, tok, 1))
        self._record(tok, reads, writes)
        return tok

    def dma(self, eng, fn, reads=(), writes=(), force=False):
        if not self.enabled and not force:
            return None
        pool_ = DMA_SEM_POOLS[eng]
        i = pool_[self.dma_next[eng]]
        self.dma_next[eng] = (self.dma_next[eng] + 1) % len(pool_)
        s = ("dma", i)
        extra = []
        if self.dma_count[i] > 0:
            extra.append((s, self.dma_count[i]))
        waits = self._deps(eng, reads, writes, extra)
        self.dma_count[i] += 16
        tok = (s, self.dma_count[i])
        self.ops[eng].append((waits, fn, tok, 16))
        self._record(tok, reads, writes)
        return tok

    def finish(self, eng, toks):
        waits = []
        for s, v in toks:
            if self.waited[eng].get(s, 0) < v:
                waits.append((s, v))
                self.waited[eng][s] = v
        self.ops[eng].append((waits, None, None, 0))

    def emit(self, block, sems):
        def run(eng_name):
            def body(e):
                for waits, fn, tok, inc in self.ops[eng_name]:
                    for s, v in waits:
                        e.wait_ge(sems[s], v)
                    if fn is None:
                        continue
                    ins = fn(e)
                    ins.then_inc(sems[tok[0]], inc)
            return body

        block.tensor(run("pe"))
        block.scalar(run("act"))
        block.vector(run("dve"))
        block.gpsimd(run("pool"))
        block.sync(run("sp"))


def build_program(dbg=False, n_quarters=NQ, n_layers=DEPTH, stop_at=None):
    nc = bass.Bass("TRN2", target_bir_lowering=False)

    def din(name, shape, dt=F32):
        return nc.dram_tensor(name, list(shape), dt, kind="ExternalInput").ap()

    x_d = din("x", [L, D])
    c_d = din("c", [128, 8])
    pos_d = din("pos", [128, 16], I32)
    w_in_d = din("w_in", [DEPTH, D, IN_W])
    w_ada_d = din("w_ada", [DEPTH, D, 3 * D])
    w_brr_d = din("w_br_ret", [DEPTH, 512, D])
    w_brs_d = din("w_br_ssm", [DEPTH, D, D])
    w_brc_d = din("w_br_sc", [DEPTH, 512, D])
    w_out_d = din("w_out", [DEPTH, D, D])
    ng_d = din("norm_g_fm", [DEPTH, 128, 8])
    bada_d = din("b_ada_bc", [DEPTH, 128, 3 * D])
    cw_d = din("conv_w_fm", [DEPTH, 128, 48])
    cb_d = din("conv_b_row", [DEPTH, 1, 1536])
    dtb_d = din("dt_bias_bc", [DEPTH, 128, 16])
    alog_d = din("a_log_bc", [DEPTH, 128, 16])
    dfm_d = din("d_fm", [DEPTH, 128, 8])
    sng_d = din("ssm_norm_g_bc", [DEPTH, 128, D])
    scw_d = din("sc_w_fm", [DEPTH, 128, 12])
    fg_d = din("final_g_bc", [128, D])
    cf_d = din("cf32", [128, C_TOT])
    cb16_d = din("cbf16", [128, 256], BF16)
    out_d = nc.dram_tensor("out", [L, D], F32, kind="ExternalOutput").ap()
    dbg_d = {}

    P = Plan()
    P.stop_at = stop_at
    with ExitStack() as st:
        sems = {}
        for e in ENGS:
            sems[e] = st.enter_context(nc.semaphore("s_" + e))
        for i in range(N_DMA_SEMS):
            sems[("dma", i)] = st.enter_context(nc.semaphore("s_dma%d" % i))

        def sb(name, cols, dt=F32, parts=128):
            return st.enter_context(nc.sbuf_tensor(name, [parts, cols], dt))

        def V3(ap, a):
            return ap.rearrange("p (a b) -> p a b", a=a)

        def V4(ap, a, b):
            return ap.rearrange("p (a b c) -> p a b c", a=a, b=b)

        xq = sb("xq", NCQ * D)
        hT = sb("hT", 8 * 512, BF16)
        merged = sb("merged", NCQ * D, BF16)
        ytemp = sb("ytemp", 8 * 512, BF16)
        slots = [sb("wslot0", SLOT_COLS, BF16), sb("wslot1", SLOT_COLS, BF16)]
        ret_f = [sb("ret_f%d" % l, 512) for l in range(DEPTH)]
        ssd_f = [sb("ssd_f%d" % l, 1024) for l in range(DEPTH)]
        ret_bf = sb("ret_bf", 512, BF16)
        ssd_bf = sb("ssd_bf", 1024, BF16)
        xtail = [sb("xtail%d" % l, 36, BF16) for l in range(DEPTH)]
        utail = [sb("utail%d" % l, 8, BF16) for l in range(DEPTH)]
        cf = sb("cf", C_TOT)
        cb16 = sb("cb16", 256, BF16)
        rg_bc = [sb("rg_bc%d" % l, D) for l in range(DEPTH)]
        gbc = sb("gbc", D)
        cossin = sb("cossin", 512)
        gs = [sb("gs%d" % l, 8) for l in range(DEPTH)]
        sh = [sb("sh%d" % l, 8) for l in range(DEPTH)]
        A_bc = [sb("A_bc%d" % l, 16) for l in range(DEPTH)]
        dtb = [sb("dtb%d" % l, 16) for l in range(DEPTH)]
        cw = [sb("cw%d" % l, 48) for l in range(DEPTH)]
        scw = [sb("scw%d" % l, 12) for l in range(DEPTH)]
        dfm = [sb("dfm%d" % l, 8) for l in range(DEPTH)]
        ngf = [sb("ngf%d" % l, 8) for l in range(DEPTH)]
        cbrow = sb("cbrow", 1536, BF16, parts=1)
        onesrow = sb("onesrow", 128, BF16, parts=1)
        diagw = sb("diagw", 48 * 128, BF16)
        diagsc = diagw
        dmt = sb("dmt", 8 * 128, BF16)
        posf = sb("posf", 16)
        cact = sb("cact", 8)
        small = sb("small", 512)
        W = [sb("W4_%d" % i, 1024) for i in range(6)]
        xpad = sb("xpad", 12 * 131, BF16)
        xc = sb("xc", 12 * 128, BF16)
        misc1 = sb("misc1", 512, BF16)
        ps = st.enter_context(nc.psum_tensor("ps", [128, 4096], F32))
        block = st.enter_context(nc.Block())

        def bank(b, n=1):
            return ps[:, b * 512:(b + n) * 512]

        def bankb(b):
            return ps[:, b * 512:(b + 1) * 512].bitcast(BF16)

        def PK(*bs):
            return [("ps", b) for b in bs]

        def SK(s):
            return [("slot", s, j) for j in range(NP_SLOT)]

        def wk(i, a=0, b=1024):
            return [("W", i, gq) for gq in range(a // 256, (b + 255) // 256)]

        identb = cb16[:, 0:128]
        xgt = cb16[:, 128:256]
        Uf = cf[:, C_U:C_U + 128]
        onesf = cf[:, C_ONES:C_ONES + 128]
        cmask = cf[:, C_CMASK:C_CMASK + 128]
        maskT = cf[:, C_MASKT:C_MASKT + 512]
        qdecT = cf[:, C_QDEC:C_QDEC + 512]
        kdec = cf[:, C_KDEC:C_KDEC + 4]
        invf = cf[:, C_INVF:C_INVF + 64]
        nhalf = cf[:, C_NHALF:C_NHALF + 16]
        twopi = cf[:, C_TWOPI:C_TWOPI + 1].to_broadcast([128, 256])

        dbg_list = []

        def dump(name, ap, cols, dt=F32, reads=()):
            if not dbg or not P.enabled:
                return
            t = nc.dram_tensor("dbg_" + name, [128, cols], dt, kind="ExternalOutput").ap()
            tok = P.dma("sp", lambda e: e.dma_start(out=t, in_=ap), reads=list(reads))
            dbg_list.append(tok)

        P.dma("sp", lambda e: e.dma_start(out=cf[:, :], in_=cf_d), writes=["cf"])
        P.dma("sp", lambda e: e.dma_start(out=cb16[:, :], in_=cb16_d), writes=["cb16"])
        ctmp = W[0][:, 0:8]
        P.dma("sp", lambda e: e.dma_start(out=ctmp, in_=c_d), writes=[*wk(0)])
        posi = W[1][:, 0:16].bitcast(I32)
        P.dma("sp", lambda e: e.dma_start(out=posi, in_=pos_d), writes=[*wk(1)])
        P.mark("s1")
        for l in range(DEPTH):
            for (t_, d_, nm) in ((A_bc[l], alog_d[l], "A_bc"), (dtb[l], dtb_d[l], "dtb"), (cw[l], cw_d[l], "cw"),
                                 (scw[l], scw_d[l], "scw"), (dfm[l], dfm_d[l], "dfm"), (ngf[l], ng_d[l], "ngf")):
                P.dma("sp", lambda e, t_=t_, d_=d_: e.dma_start(out=t_[:, :], in_=d_), writes=[(nm, l)])
            P.op("act", lambda e, l=l: e.activation(out=A_bc[l][:, :], in_=A_bc[l][:, :], func=AF.Exp),
                 reads=[("A_bc", l)], writes=[("A_bc", l)])
            P.op("dve", lambda e, l=l: e.tensor_scalar(out=A_bc[l][:, :], in0=A_bc[l][:, :], scalar1=-1.0, scalar2=None,
                                                       op0=ALU.mult), reads=[("A_bc", l)], writes=[("A_bc", l)])
            P.op("dve", lambda e, l=l: e.tensor_scalar(out=scw[l][:, :], in0=scw[l][:, :], scalar1=0.5, scalar2=None,
                                                       op0=ALU.mult), reads=[("scw", l)], writes=[("scw", l)])
            P.op("dve", lambda e, l=l: e.tensor_scalar(out=dfm[l][:, :], in0=dfm[l][:, :], scalar1=0.5, scalar2=None,
                                                       op0=ALU.mult), reads=[("dfm", l)], writes=[("dfm", l)])
            for t_, nm in ((ret_f[l], "ret_f"), (ssd_f[l], "ssd_f")):
                P.op("dve", lambda e, t_=t_: e.memset(t_[:, :], 0.0), writes=[(nm, l)])
            P.op("dve", lambda e, l=l: e.memset(xtail[l][:, :], 0.0), writes=[("xtail", l)])
            P.op("dve", lambda e, l=l: e.memset(utail[l][:, :], 0.0), writes=[("utail", l)])
        P.mark("s2")
        P.op("dve", lambda e: e.memset(onesrow[:, :], 1.0), writes=["onesrow"])
        P.op("dve", lambda e: e.tensor_copy(out=posf[:, :], in_=posi), reads=[*wk(1)], writes=["posf"])
        tct = W[0][:, 8:16]
        hct = W[0][:, 16:24]
        P.op("act", lambda e: e.activation(out=tct, in_=ctmp, func=AF.Tanh, scale=0.5), reads=[*wk(0)], writes=[*wk(0,0,256)])
        P.op("dve", lambda e: e.tensor_scalar(out=hct, in0=ctmp, scalar1=0.5, scalar2=None, op0=ALU.mult),
             reads=[*wk(0)], writes=[*wk(0,0,256)])
        P.op("dve", lambda e: e.scalar_tensor_tensor(out=cact[:, :], in0=tct, scalar=1.0, in1=hct, op0=ALU.add,
                                                     op1=ALU.mult), reads=[*wk(0,0,256), *wk(0,0,256)], writes=["cact"])
        crep = W[2][:, :]
        P.op("dve", lambda e: e.tensor_copy(out=V3(crep, 8), in_=cact[:, :].unsqueeze(2).to_broadcast([128, 8, 128])),
             reads=["cact"], writes=[*wk(2)])

        P.mark("s3")
        for l in range(DEPTH):
            for n in range(6):
                s = n % 2
                wv = V3(slots[s][:, 0:8192].bitcast(F32), 8)
                wsrc = w_ada_d[l][:, n * 512:(n + 1) * 512].rearrange("(k p) c -> p k c", p=128)
                for kk in range(8):
                    P.dma("sp", lambda e, wv=wv, wsrc=wsrc, kk=kk: e.dma_start(out=wv[:, kk, :], in_=wsrc[:, kk, :]),
                          writes=[("slot", s, kk)])
                bb = W[3][:, 0:512]
                P.dma("sp", lambda e, l=l, n=n, bb=bb: e.dma_start(out=bb, in_=bada_d[l][:, n * 512:(n + 1) * 512]),
                      writes=[*wk(3)])
                pb = n % 2

                def mm(e, wv=wv, pb=pb):
                    for kk in range(8):
                        ins = e.matmul(bank(pb), lhsT=V3(crep, 8)[:, kk, :], rhs=wv[:, kk, :], start=(kk == 0),
                                       stop=(kk == 7))
                    return ins
                P.op("pe", mm, reads=[*wk(2)] + SK(s), writes=PK(pb))
                P.mark("ada_mm_%d_%d" % (l, n))
                if n >= 4:
                    dst = rg_bc[l][:, (n - 4) * 512:(n - 3) * 512]
                    P.op("dve", lambda e, dst=dst, pb=pb, bb=bb: e.tensor_tensor(out=dst, in0=bank(pb), in1=bb, op=ALU.add),
                         reads=PK(pb) + [*wk(3)], writes=[("rg_bc", l)])
                    P.op("dve", lambda e, dst=dst: e.tensor_scalar(out=dst, in0=dst, scalar1=0.5, scalar2=None, op0=ALU.mult),
                         reads=[("rg_bc", l)], writes=[("rg_bc", l)])
                else:
                    blk = W[4][:, 0:512]
                    P.op("dve", lambda e, blk=blk, pb=pb, bb=bb: e.tensor_tensor(out=blk, in0=bank(pb), in1=bb, op=ALU.add),
                         reads=PK(pb) + [*wk(3)], writes=[*wk(4)])
                    P.op("dve", lambda e, blk=blk: e.tensor_tensor(out=V3(blk, 4), in0=V3(blk, 4),
                                                                   in1=identb.unsqueeze(1).to_broadcast([128, 4, 128]),
                                                                   op=ALU.mult), reads=[*wk(4), "cb16"], writes=[*wk(4)])
                    tgt = (sh[l] if n < 2 else gs[l])[:, (n % 2) * 4:(n % 2) * 4 + 4]
                    P.op("dve", lambda e, blk=blk, tgt=tgt: e.tensor_reduce(out=tgt, in_=V3(blk, 4), axis=AX.X, op=ALU.add),
                         reads=[*wk(4)], writes=[("shgs", l)])
            P.op("dve", lambda e, l=l: e.scalar_tensor_tensor(out=gs[l][:, :], in0=gs[l][:, :], scalar=1.0, in1=ngf[l][:, :],
                                                              op0=ALU.add, op1=ALU.mult),
                 reads=[("shgs", l), ("ngf", l)], writes=[("shgs", l)])
        P.mark("adaend")
        if dbg:
            dump("gs0", gs[0][:, :], 8, reads=[("shgs", 0)])
            dump("sh0", sh[0][:, :], 8, reads=[("shgs", 0)])
            dump("rg0", rg_bc[0][:, :], D, reads=[("rg_bc", 0)])

        P.mark("setup")
        GROUPS = ["A1", "A2", "B1", "B2", "C1", "C2"]
        seq = [(q, l, g) for q in range(n_quarters) for l in range(n_layers) for g in GROUPS]
        slot_of = {key: i % 2 for i, key in enumerate(seq)}

        GIDX = {g_: i_ for i_, g_ in enumerate(GROUPS)}
        USED = {"A1": 8 * 2048, "A2": 8192 + 4096, "B1": 8 * 2576, "B2": 16384, "C1": 16384, "C2": 12288 + 8192}
        wscr = nc.dram_tensor("wscr", [DEPTH * 6 * 128, SLOT_COLS], BF16).ap()

        def load_group(q, l, g):
            s = slot_of[(q, l, g)]
            sl = slots[s]
            used = USED[g]
            r0 = (l * 6 + GIDX[g]) * 128
            scr = wscr[r0:r0 + 128, :]
            if q > 0:
                nsp = 4
                w_ = used // nsp
                for j in range(nsp):
                    P.dma("sp", lambda e, j=j: e.dma_start(out=sl[:, j * w_:(j + 1) * w_], in_=scr[:, j * w_:(j + 1) * w_]),
                          reads=[("scr", l, g)], writes=[("slot", s, j)])
                return s
            pieces = []

            def add(col0, src, nk, cols):
                dst = sl[:, col0:col0 + nk * cols].rearrange("p (k c) -> p k c", k=nk)
                pieces.append((dst, src.rearrange("(k p) c -> p k c", p=128), nk))
            if g == "A1":
                add(0, w_in_d[l][:, 0:2048], 8, 2048)
            elif g == "A2":
                add(0, w_in_d[l][:, O_G0:O_G0 + 1024], 8, 1024)
                add(8192, w_brr_d[l], 4, 1024)
            elif g == "B1":
                add(0, w_in_d[l][:, O_Z:O_Z + 2576], 8, 2576)
            elif g == "B2":
                add(0, w_in_d[l][:, O_G1:O_G1 + 1024], 8, 1024)
                add(8192, w_brs_d[l], 8, 1024)
            elif g == "C1":
                add(0, w_in_d[l][:, O_CB:O_CB + 2048], 8, 2048)
            elif g == "C2":
                add(0, w_in_d[l][:, O_G2:O_G2 + 1024], 8, 1024)
                add(8192, w_brc_d[l], 4, 1024)
                add(12288, w_out_d[l], 8, 1024)
            j = 0
            for dst, src, nk in pieces:
                for kk in range(nk):
                    P.dma("pool", lambda e, dst=dst, src=src, kk=kk: e.dma_start(out=dst[:, kk, :], in_=src[:, kk, :]),
                          writes=[("slot", s, j)])
                    j += 1
            if n_quarters > 1:
                P.dma("sp", lambda e: e.dma_start(out=scr[:, 0:used], in_=sl[:, 0:used]),
                      reads=[("slot", s, jj) for jj in range(j)], writes=[("scr", l, g)])
            return s

        def prefetch_next(q, l, g, early=False):
            i = seq.index((q, l, g))
            if i + 1 >= len(seq):
                return
            nq = seq[i + 1][0]
            if g in ("A2", "B2", "C2"):
                if early != (nq > 0):
                    return
            load_group(*seq[i + 1])

        def rsqrt_small(dst, src, n, rk, wk):
            P.op("pool", lambda e: e.tensor_tensor(out=dst, in0=src, in1=nhalf[:, 0:n], op=ALU.pow),
                 reads=list(rk) + ["cf"], writes=list(wk))

        out_toks = []
        first_group_loaded = False
        xq3 = V3(xq[:, :], NCQ)
        for q in range(n_quarters):
            if q == 0:
                for t in range(NCQ):
                    r0 = t * 128
                    P.dma("sp", lambda e, t=t, r0=r0: e.dma_start(out=xq3[:, t, :], in_=x_d[r0:r0 + 128, :]),
                          writes=[("xq", t)])
                load_group(*seq[0])

            def final_cb(t, q=q):
                fssq, fms, frs = small[:, 44 + t:45 + t], small[:, 48 + t:49 + t], small[:, 52 + t:53 + t]
                P.op("act", lambda e: e.activation(out=xpad[:, 0:1024], in_=xq3[:, t, :], func=AF.Square, accum_out=fssq),
                     reads=[("xq", t)], writes=["xpad_h", ("xpad_b", 0), ("xpad_b", 1), ("xpad_b", 2), ("fssq", t)])
                P.op("dve", lambda e: e.tensor_scalar(out=fms, in0=fssq, scalar1=1.0 / D, scalar2=EPS, op0=ALU.mult, op1=ALU.add),
                     reads=[("fssq", t)], writes=[("fms", t)])
                rsqrt_small(frs, fms, 1, [("fms", t)], [("frs", t)])
                ob = W[4 + (t % 2)]
                P.op("dve", lambda e: e.scalar_tensor_tensor(out=ob[:, :], in0=xq3[:, t, :], scalar=frs, in1=gbc[:, :],
                                                             op0=ALU.mult, op1=ALU.mult),
                     reads=[("xq", t), ("frs", t), "gbc"], writes=wk(4 + (t % 2)))
                r0 = q * 512 + t * 128
                tok = P.dma("sp", lambda e: e.dma_start(out=out_d[r0:r0 + 128, :], in_=ob[:, :]), reads=wk(4 + (t % 2)))
                out_toks.append(tok)
                if q + 1 < n_quarters:
                    r1 = (q + 1) * 512 + t * 128
                    P.dma("sp", lambda e: e.dma_start(out=xq3[:, t, :], in_=x_d[r1:r1 + 128, :]), writes=[("xq", t)])

            for l in range(n_layers):
                emit_layer_quarter(nc, P, locals(), q, l)

        P.finish("sp", [t_ for t_ in out_toks + dbg_list if t_ is not None])
        P.emit(block, sems)
    return nc


def emit_layer_quarter(nc, P, env, q, l):
    g = env
    (xq, hT, merged, ytemp, slots, ret_f, ssd_f, ret_bf, ssd_bf, xtail, utail, cf, cb16, cossin, rg_bc, gbc, gs, sh,
     A_bc, dtb, cw, scw, dfm, cbrow, onesrow, diagw, diagsc, dmt, small, W, xpad, xc, misc1, ps) = [
        g[k] for k in ("xq", "hT", "merged", "ytemp", "slots", "ret_f", "ssd_f", "ret_bf", "ssd_bf", "xtail", "utail",
                       "cf", "cb16", "gbc", "rg_bc", "gbc", "gs", "sh", "A_bc", "dtb", "cw", "scw", "dfm", "cbrow",
                       "onesrow", "diagw", "diagsc", "dmt", "small", "W", "xpad", "xc", "misc1", "ps")]
    cossin = g["cossin"]
    bank, bankb, PK, SK, V3, V4 = g["bank"], g["bankb"], g["PK"], g["SK"], g["V3"], g["V4"]
    identb, xgt, Uf, onesf, cmask, maskT, qdecT, kdec, nhalf = [g[k] for k in (
        "identb", "xgt", "Uf", "onesf", "cmask", "maskT", "qdecT", "kdec", "nhalf")]
    slot_of, prefetch_next, rsqrt_small, dump, dbg, sng_d = (g["slot_of"], g["prefetch_next"], g["rsqrt_small"],
                                                             g["dump"], g["dbg"], g["sng_d"])
    wk = g["wk"]
    HTK = ["hT"] + [("hTk", k_) for k_ in range(8)]
    D3 = dbg and q == 0 and l == 0
    xq3 = V3(xq[:, :], NCQ)
    hT3 = V3(hT[:, :], 8)
    mg3 = V3(merged[:, :], NCQ)
    yT3 = V3(ytemp[:, :], 8)
    xn3 = V3(ytemp[:, :], NCQ)

    def emit_rotary(qq):
        invf = g["invf"]
        twopi = g["twopi"]
        posf = g["posf"]
        ang = W[5][:, 0:256]
        a1 = W[5][:, 256:512]
        a2 = W[5][:, 512:768]
        P.op("dve", lambda e: e.tensor_tensor(
            out=V3(ang, 4), in0=posf[:, qq * 4:(qq + 1) * 4].unsqueeze(2).to_broadcast([128, 4, 64]),
            in1=invf.unsqueeze(1).to_broadcast([128, 4, 64]), op=ALU.mult), reads=["posf", "cf"], writes=wk(5, 0, 256))
        MAGIC = 12582912.0
        C1 = 6.28125
        C2 = 2.0 * math.pi - 6.28125
        PI_LO = 3.1415925
        kf = W[5][:, 768:1024]
        P.op("dve", lambda e: e.tensor_scalar(out=kf, in0=ang, scalar1=1.0 / (2.0 * math.pi), scalar2=MAGIC, op0=ALU.mult,
                                              op1=ALU.add), reads=wk(5, 0, 256), writes=wk(5, 768, 1024))
        P.op("dve", lambda e: e.tensor_scalar(out=kf, in0=kf, scalar1=-MAGIC, scalar2=None, op0=ALU.add),
             reads=wk(5, 768, 1024), writes=wk(5, 768, 1024))
        P.op("dve", lambda e: e.scalar_tensor_tensor(out=a1, in0=kf, scalar=-C1, in1=ang, op0=ALU.mult, op1=ALU.add),
             reads=wk(5, 768, 1024) + wk(5, 0, 256), writes=wk(5, 256, 512))
        P.op("dve", lambda e: e.scalar_tensor_tensor(out=a1, in0=kf, scalar=-C2, in1=a1, op0=ALU.mult, op1=ALU.add),
             reads=wk(5, 768, 1024) + wk(5, 256, 512), writes=wk(5, 256, 512))
        P.op("dve", lambda e: e.tensor_scalar(out=a2, in0=a1, scalar1=0.5 * math.pi, scalar2=None, op0=ALU.add),
             reads=wk(5, 256, 512), writes=wk(5, 512, 768))
        P.op("dve", lambda e: e.tensor_scalar(out=kf, in0=a2, scalar1=math.pi, scalar2=-2.0 * math.pi, op0=ALU.is_gt,
                                              op1=ALU.mult), reads=wk(5, 512, 768), writes=wk(5, 768, 1024))
        P.op("dve", lambda e: e.tensor_tensor(out=a2, in0=a2, in1=kf, op=ALU.add), reads=wk(5, 512, 768) + wk(5, 768, 1024),
             writes=wk(5, 512, 768))
        P.op("dve", lambda e: e.tensor_scalar(out=a1, in0=a1, scalar1=-PI_LO, scalar2=PI_LO, op0=ALU.max, op1=ALU.min),
             reads=wk(5, 256, 512), writes=wk(5, 256, 512))
        P.op("dve", lambda e: e.tensor_scalar(out=a2, in0=a2, scalar1=-PI_LO, scalar2=PI_LO, op0=ALU.max, op1=ALU.min),
             reads=wk(5, 512, 768), writes=wk(5, 512, 768))
        P.op("act", lambda e: e.activation(out=cossin[:, 256:512], in_=a1, func=AF.Sin), reads=wk(5, 256, 512), writes=["cossin"])
        P.op("act", lambda e: e.activation(out=cossin[:, 0:256], in_=a2, func=AF.Sin), reads=wk(5, 512, 768), writes=["cossin"])
        if D3:
            dump("cossin", cossin[:, 0:512], 512, reads=["cossin"])

    def emit_prep(ll):
        P.dma("pool", lambda e: e.dma_start(out=cbrow[:, :], in_=g["cb_d"][ll]), writes=["cbrow"])

        dg4 = V4(diagw[:, :], 12, 4)
        P.op("pool", lambda e: e.tensor_tensor(out=V3(diagw[:, :], 48), in0=identb.unsqueeze(1).to_broadcast([128, 48, 128]),
                                               in1=cw[ll][:, :].unsqueeze(2).to_broadcast([128, 48, 128]), op=ALU.mult),
             reads=["cb16", ("cw", ll)], writes=["diagw"])
        dm3 = V3(dmt[:, :], 8)
        P.op("pool", lambda e: e.tensor_tensor(out=dm3, in0=identb.unsqueeze(1).to_broadcast([128, 8, 128]),
                                               in1=dfm[ll][:, :].unsqueeze(2).to_broadcast([128, 8, 128]), op=ALU.mult),
             reads=["cb16", ("dfm", ll)], writes=["dmt"])

    dg4 = V4(diagw[:, :], 12, 4)
    dm3 = V3(dmt[:, :], 8)
    n_layers_, n_quarters_ = g["n_layers"], g["n_quarters"]
    if q == 0 and l == 0:
        emit_rotary(0)
        emit_prep(0)
    ssq = small[:, 0:4]
    ms = small[:, 4:8]
    rstd = small[:, 8:12]
    for t in range(NCQ):
        P.op("act", lambda e, t=t: e.activation(out=W[0][:, 0:512].bitcast(BF16), in_=xq3[:, t, :], func=AF.Square,
                                                accum_out=ssq[:, t:t + 1]), reads=[("xq", t)], writes=[*wk(0), ("ssq", t)])
    for (c0, c1, tag) in ((0, NCQ - 1, "A"), (NCQ - 1, NCQ, "B")):
        P.op("dve", lambda e, c0=c0, c1=c1: e.tensor_scalar(out=ms[:, c0:c1], in0=ssq[:, c0:c1], scalar1=1.0 / D, scalar2=EPS,
                                                            op0=ALU.mult, op1=ALU.add),
             reads=[("ssq", t) for t in range(c0, c1)], writes=["ms" + tag])
        rsqrt_small(rstd[:, c0:c1], ms[:, c0:c1], c1 - c0, ["ms" + tag], ["rstd" + tag])
        for t in range(c0, c1):
            P.op("dve", lambda e, t=t: e.tensor_scalar(out=xn3[:, t, :], in0=xq3[:, t, :], scalar1=rstd[:, t:t + 1],
                                                       scalar2=None, op0=ALU.mult), reads=[("xq", t), "rstd" + tag], writes=["ytemp"])
    for kk in range(8):
        pb = kk % 4

        def tr(e, kk=kk, pb=pb):
            for t in range(NCQ):
                ins = e.transpose(out=bankb(pb)[:, t * 128:(t + 1) * 128], in_=xn3[:, t, kk * 128:(kk + 1) * 128],
                                  identity=identb)
            return ins
        P.op("pe", tr, reads=["ytemp", "cb16"], writes=PK(pb))
        if kk % 2 == 0:
            P.op("act", lambda e, kk=kk, pb=pb: e.activation(out=hT3[:, kk, :], in_=bankb(pb)[:, 0:512], func=AF.Identity,
                                                              scale=gs[l][:, kk:kk + 1], bias=sh[l][:, kk:kk + 1]),
                 reads=PK(pb) + [("shgs", l)], writes=[("hTk", kk)])
        else:
            P.op("dve", lambda e, kk=kk, pb=pb: e.tensor_scalar(out=hT3[:, kk, :], in0=bankb(pb)[:, 0:512],
                                                                scalar1=gs[l][:, kk:kk + 1], scalar2=sh[l][:, kk:kk + 1],
                                                                op0=ALU.mult, op1=ALU.add),
                 reads=PK(pb) + [("shgs", l)], writes=[("hTk", kk)])
    if D3:
        dump("hT", hT[:, :], 4096, BF16, reads=HTK)

    P.op("act", lambda e: e.activation(out=ret_bf[:, :], in_=ret_f[l][:, :], func=AF.Copy), reads=[("ret_f", l)],
         writes=["ret_bf"])
    P.op("act", lambda e: e.activation(out=ssd_bf[:, :], in_=ssd_f[l][:, :], func=AF.Copy), reads=[("ssd_f", l)],
         writes=["ssd_bf"])

    P.mark("H")
    s = slot_of[(q, l, "A1")]
    prefetch_next(q, l, "A1")
    wA = V3(slots[s][:, 0:8 * 2048], 8)
    cos_q = V3(cossin[:, 0:256], 4)
    sin_q = V3(cossin[:, 256:512], 4)
    gam = [1.0 - 2.0 ** (-5 - h) for h in range(4)]
    cdec = [gm ** 128 for gm in gam]
    ret4 = V3(ret_f[l][:, :], 4)
    retb4 = V3(ret_bf[:, :], 4)
    qk4 = V4(ps[:, 0:1024], 8, 2)
    ta = V4(W[0][:, :], 8, 2)
    tb = V4(W[1][:, :], 8, 2)
    rot = V4(W[2][:, 0:512].bitcast(BF16), 8, 2)
    rot3 = V3(W[2][:, 0:512].bitcast(BF16), 8)
    qkT = W[2][:, 512:1024].bitcast(BF16)
    qT = V3(qkT[:, 0:512], 4)
    kT = V3(qkT[:, 512:1024], 4)
    vv = W[3][:, 0:512].bitcast(BF16)
    vb = V3(vv[:, 0:512], 4)
    vd = V3(vv[:, 512:1024], 4)
    sT = V3(W[3][:, 512:768].bitcast(BF16), 4)
    osb = W[4][:, 0:512]
    sqj = W[4][:, 512:768].bitcast(BF16)
    tg = W[5][:, 0:256].bitcast(BF16)
    sgs = [W[5][:, 256:512].bitcast(BF16), W[5][:, 512:768].bitcast(BF16)]
    sgk = [wk(5, 256, 512), wk(5, 512, 768)]
    yret = W[5][:, 768:1024].bitcast(BF16)
    ssum, ssqh, mean, var, rs4, nmr = (small[:, 16:20], small[:, 20:24], small[:, 24:28], small[:, 28:32],
                                       small[:, 32:36], small[:, 36:40])

    def a1_piece(t, n):
        tok = slice(t * 128, (t + 1) * 128)

        def proj(e):
            for kk in range(8):
                ins = e.matmul(bank(n), lhsT=hT3[:, kk, tok], rhs=wA[:, kk, n * 512:(n + 1) * 512], start=(kk == 0),
                               stop=(kk == 7))
            return ins
        P.op("pe", proj, reads=HTK + SK(s), writes=PK(n))

    def a1_E(t):
        sg = sgs[t % 2]
        P.op("act", lambda e: e.activation(out=tg, in_=bank(3), func=AF.Tanh, scale=0.5), reads=PK(3), writes=[*wk(5,0,256)])
        P.op("act", lambda e: e.activation(out=vv[:, 0:512], in_=bank(2), func=AF.Copy), reads=PK(2), writes=[*wk(3,0,256)])
        cb_ = cos_q[:, t, :].unsqueeze(1).unsqueeze(1).to_broadcast([128, 8, 2, 64])
        sb_ = sin_q[:, t, :].unsqueeze(1).unsqueeze(1).to_broadcast([128, 8, 2, 64])
        P.op("dve", lambda e: e.tensor_tensor(out=ta, in0=qk4, in1=cb_, op=ALU.mult), reads=PK(0, 1) + ["cossin"], writes=[*wk(0)])
        P.op("dve", lambda e: e.tensor_tensor(out=tb, in0=qk4, in1=sb_, op=ALU.mult), reads=PK(0, 1) + ["cossin"], writes=[*wk(1)])
        P.op("dve", lambda e: e.tensor_tensor(out=rot[:, :, 0, :], in0=ta[:, :, 0, :], in1=tb[:, :, 1, :], op=ALU.subtract),
             reads=[*wk(0), *wk(1)], writes=[*wk(2,0,512)])
        P.op("dve", lambda e: e.tensor_tensor(out=rot[:, :, 1, :], in0=tb[:, :, 0, :], in1=ta[:, :, 1, :], op=ALU.add),
             reads=[*wk(0), *wk(1)], writes=[*wk(2,0,512)])

        def trqk(e):
            for j in range(8):
                ins = e.transpose(out=bankb(4)[:, j * 128:(j + 1) * 128], in_=rot3[:, j, :], identity=identb)
            return ins
        P.op("pe", trqk, reads=[*wk(2,0,512), "cb16"], writes=PK(4))
        P.op("dve", lambda e: e.tensor_tensor(out=vd, in0=V3(bank(2), 4), in1=kdec.unsqueeze(2).to_broadcast([128, 4, 128]),
                                              op=ALU.mult), reads=PK(2) + ["cf"], writes=[*wk(3,256,512)])
        P.op("dve", lambda e: e.scalar_tensor_tensor(out=sg, in0=tg, scalar=1.0, in1=bank(3), op0=ALU.add, op1=ALU.mult),
             reads=[*wk(5,0,256)] + PK(3), writes=sgk[t % 2])
        P.op("dve", lambda e: e.tensor_tensor(out=qkT[:, 0:512], in0=bankb(4)[:, 0:512], in1=qdecT, op=ALU.mult),
             reads=PK(4) + ["cf"], writes=[*wk(2,512,768)])
        P.op("act", lambda e: e.activation(out=qkT[:, 512:1024], in_=bankb(4)[:, 512:1024], func=AF.Copy),
             reads=PK(4), writes=[*wk(2,768,1024)])

    def a1_scores(t):
        def scores(e):
            for h in range(4):
                e.matmul(bank(5)[:, h * 128:(h + 1) * 128], lhsT=kT[:, h, :], rhs=qT[:, h, :], start=True, stop=True)
            for h in range(4):
                ins = e.matmul(bank(6)[:, h * 128:(h + 1) * 128], lhsT=rot3[:, 4 + h, :], rhs=vd[:, h, :], start=True, stop=True)
            return ins
        P.op("pe", scores, reads=[*wk(2,512,768), *wk(2,768,1024), *wk(2,0,512), *wk(3,256,512)], writes=PK(5, 6))
        P.op("dve", lambda e: e.tensor_tensor(out=W[3][:, 512:768].bitcast(BF16), in0=bank(5), in1=maskT, op=ALU.mult),
             reads=PK(5) + ["cf"], writes=[*wk(3,512,768)])

    def a1_o(t):
        def omm(e):
            for h in range(4):
                e.matmul(bank(7)[:, h * 128:(h + 1) * 128], lhsT=sT[:, h, :], rhs=vb[:, h, :], start=True, stop=False)
                ins = e.matmul(bank(7)[:, h * 128:(h + 1) * 128], lhsT=qT[:, h, :], rhs=retb4[:, h, :], start=False, stop=True)
            return ins
        P.op("pe", omm, reads=[*wk(3,512,768), *wk(3,0,256), *wk(2,512,768), "ret_bf"], writes=PK(7))
        P.op("act", lambda e: e.activation(out=osb, in_=bank(7), func=AF.Copy), reads=PK(7), writes=[*wk(4,0,512)])
        for h in range(4):
            P.op("dve", lambda e, h=h: e.scalar_tensor_tensor(out=ret4[:, h, :], in0=ret4[:, h, :], scalar=float(cdec[h]),
                                                              in1=bank(6)[:, h * 128:(h + 1) * 128], op0=ALU.mult, op1=ALU.add),
                 reads=[("ret_f", l)] + PK(6), writes=[("ret_f", l)])
        P.op("act", lambda e: e.activation(out=ret_bf[:, :], in_=ret_f[l][:, :], func=AF.Copy), reads=[("ret_f", l)],
             writes=["ret_bf"])

    def a1_Ta(t):
        P.op("dve", lambda e: e.tensor_reduce(out=ssum, in_=V3(osb, 4), axis=AX.X, op=ALU.add), reads=[*wk(4,0,512)], writes=["ssum"])
        P.op("act", lambda e: e.activation(out=W[1][:, 0:512], in_=osb, func=AF.Square), reads=[*wk(4,0,512)], writes=[*wk(1,0,512)])
        P.op("dve", lambda e: e.tensor_reduce(out=ssqh, in_=V3(W[1][:, 0:512], 4), axis=AX.X, op=ALU.add), reads=[*wk(1,0,512)], writes=["ssqh"])
        P.op("dve", lambda e: e.tensor_scalar(out=mean, in0=ssum, scalar1=1.0 / 128, scalar2=None, op0=ALU.mult),
             reads=["ssum"], writes=["mean"])
        P.op("dve", lambda e: e.tensor_tensor(out=var, in0=mean, in1=mean, op=ALU.mult), reads=["mean"], writes=["var"])
        P.op("dve", lambda e: e.scalar_tensor_tensor(out=var, in0=ssqh, scalar=1.0 / 128, in1=var, op0=ALU.mult,
                                                     op1=ALU.subtract), reads=["ssqh", "var"], writes=["var"])
        P.op("dve", lambda e: e.tensor_scalar(out=var, in0=var, scalar1=EPS, scalar2=None, op0=ALU.add), reads=["var"],
             writes=["var"])
        rsqrt_small(rs4, var, 4, ["var"], ["rs4"])

    def a1_Tb(t):
        tok = slice(t * 128, (t + 1) * 128)
        sg = sgs[t % 2]
        P.op("dve", lambda e: e.tensor_scalar(out=rs4, in0=rs4, scalar1=0.5, scalar2=None, op0=ALU.mult), reads=["rs4"],
             writes=["rs4"])
        P.op("dve", lambda e: e.scalar_tensor_tensor(out=nmr, in0=mean, scalar=-1.0, in1=rs4, op0=ALU.mult, op1=ALU.mult),
             reads=["mean", "rs4"], writes=["nmr"])
        for h in range(4):
            P.op("act", lambda e, h=h: e.activation(out=osb[:, h * 128:(h + 1) * 128], in_=osb[:, h * 128:(h + 1) * 128],
                                                     func=AF.Identity, scale=rs4[:, h:h + 1], bias=nmr[:, h:h + 1]),
                 reads=[*wk(4,0,512), "rs4", "nmr"], writes=[*wk(4,0,512)])
        P.op("dve", lambda e: e.tensor_tensor(out=yret, in0=osb, in1=sg, op=ALU.mult), reads=[*wk(4,0,512)] + sgk[t % 2],
             writes=[*wk(5,768,1024)])

    def a1_Tc(t):
        tok = slice(t * 128, (t + 1) * 128)

        def tryr(e):
            for j in range(4):
                ins = e.transpose(out=bankb(4)[:, j * 128:(j + 1) * 128], in_=yret[:, j * 128:(j + 1) * 128], identity=identb)
            return ins
        P.op("pe", tryr, reads=[*wk(5,768,1024), "cb16"], writes=PK(4))
        P.op("act", lambda e: e.activation(out=yT3[:, 0:4, tok], in_=V3(bankb(4)[:, 0:512], 4), func=AF.Copy),
             reads=PK(4), writes=["ytemp"])
        if D3 and t == 0:
            dump("yret", yret, 512, BF16, reads=[*wk(5,768,1024)])

    for n in range(4):
        a1_piece(0, n)
    for t in range(NCQ):
        nxt = t + 1 < NCQ
        a1_E(t)
        if nxt:
            a1_piece(t + 1, 0)
        if t >= 1:
            a1_Ta(t - 1)
        a1_scores(t)
        if nxt:
            a1_piece(t + 1, 1)
        if t >= 1:
            a1_Tb(t - 1)
            a1_Tc(t - 1)
        a1_o(t)
        if nxt:
            a1_piece(t + 1, 2)
            a1_piece(t + 1, 3)
            if t + 1 == NCQ - 1:
                prefetch_next(q, l, "A2", early=True)

    def a1_last():
        a1_Ta(NCQ - 1)
        a1_Tb(NCQ - 1)

    def merge_pass(grp, nkb, first, last, after_tail=None, inject=None, inject_early=None):
        s2 = slot_of[(q, l, grp)]
        prefetch_next(q, l, grp)
        wg = V3(slots[s2][:, 0:8192], 8)
        wb = V3(slots[s2][:, 8192:8192 + nkb * 1024], nkb)
        wo = V3(slots[s2][:, 12288:12288 + 8192], 8)
        unit = [0]

        def tail(t):
            def trm(e):
                for j in range(8):
                    ins = e.transpose(out=bankb(4)[:, j * 128:(j + 1) * 128], in_=mg3[:, t, j * 128:(j + 1) * 128],
                                      identity=identb)
                return ins
            P.op("pe", trm, reads=[("mg", t, 0), ("mg", t, 1), "cb16"], writes=PK(4))
            mT = V3(W[2][:, 0:512].bitcast(BF16), 8)
            P.op("act", lambda e: e.activation(out=W[2][:, 0:512].bitcast(BF16), in_=bankb(4), func=AF.Copy),
                 reads=PK(4), writes=[*wk(2, 0, 512)])

            def om(e):
                for n in range(2):
                    for kk in range(8):
                        ins = e.matmul(bank(5 + n), lhsT=mT[:, kk, :], rhs=wo[:, kk, n * 512:(n + 1) * 512],
                                       start=(kk == 0), stop=(kk == 7))
                return ins
            P.op("pe", om, reads=[*wk(2, 0, 512)] + SK(s2), writes=PK(5, 6))
            tmp2 = W[3][:, :]
            P.op("dve", lambda e: e.tensor_tensor(out=tmp2, in0=ps[:, 2560:3584], in1=rg_bc[l][:, :], op=ALU.mult),
                 reads=PK(5, 6) + [("rg_bc", l)], writes=[*wk(3)])
            P.op("dve", lambda e: e.tensor_tensor(out=xq3[:, t, :], in0=xq3[:, t, :], in1=tmp2, op=ALU.add),
                 reads=[*wk(3), ("xq", t)], writes=[("xq", t)])

        if inject_early is not None:
            inject_early()
        for t in range(NCQ):
            tok = slice(t * 128, (t + 1) * 128)
            if t == 3 and inject is not None:
                inject()
            for n in range(2):
                nrot = 2 if last else 4
                par = unit[0] % nrot
                unit[0] += 1
                gb, pbk = 2 * par, 2 * par + 1

                def gp(e, tok=tok, n=n, gb=gb, pbk=pbk):
                    for kk in range(8):
                        e.matmul(bank(gb), lhsT=hT3[:, kk, tok], rhs=wg[:, kk, n * 512:(n + 1) * 512], start=(kk == 0),
                                 stop=(kk == 7))
                    for kk in range(nkb):
                        ins = e.matmul(bank(pbk), lhsT=yT3[:, kk, tok], rhs=wb[:, kk, n * 512:(n + 1) * 512],
                                       start=(kk == 0), stop=(kk == nkb - 1))
                    return ins
                P.op("pe", gp, reads=HTK + ["ytemp"] + SK(s2), writes=PK(gb, pbk))
                tgate = W[0][:, par * 256:(par + 1) * 256].bitcast(BF16)
                tgk = wk(0, par * 256, par * 256 + 256)
                P.op("act", lambda e, tgate=tgate, gb=gb: e.activation(out=tgate, in_=bank(gb), func=AF.Tanh, scale=0.5),
                     reads=PK(gb), writes=tgk)
                mgs = mg3[:, t, n * 512:(n + 1) * 512]
                if first:
                    P.op("dve", lambda e, tgate=tgate, pbk=pbk, mgs=mgs: e.scalar_tensor_tensor(
                        out=mgs, in0=tgate, scalar=1.0, in1=bank(pbk), op0=ALU.add, op1=ALU.mult),
                        reads=tgk + PK(pbk), writes=[("mg", t, n)])
                else:
                    tmp = W[1 + par // 2][:, (par % 2) * 512:(par % 2 + 1) * 512]
                    tmk = wk(1 + par // 2, (par % 2) * 512, (par % 2) * 512 + 512)
                    P.op("dve", lambda e, tgate=tgate, pbk=pbk, tmp=tmp: e.scalar_tensor_tensor(
                        out=tmp, in0=tgate, scalar=1.0, in1=bank(pbk), op0=ALU.add, op1=ALU.mult),
                        reads=tgk + PK(pbk), writes=tmk)
                    P.op("dve", lambda e, mgs=mgs, tmp=tmp: e.tensor_tensor(out=mgs, in0=mgs, in1=tmp, op=ALU.add),
                         reads=tmk + [("mg", t, n)], writes=[("mg", t, n)])
            if last and t >= 1:
                tail(t - 1)
                if after_tail is not None:
                    after_tail(t - 1)
        if last:
            tail(NCQ - 1)
            if after_tail is not None:
                after_tail(NCQ - 1)

    P.mark("A1")
    merge_pass("A2", 4, True, False, inject_early=a1_last, inject=lambda: a1_Tc(NCQ - 1))
    P.mark("A2")
    if D3:
        dump("mgA", merged[:, 0:1024], 1024, BF16, reads=[("mg", 0, 0), ("mg", 0, 1)])

    P.dma("sp", lambda e: e.dma_start(out=gbc[:, :], in_=sng_d[l]), writes=["gbc"])
    s = slot_of[(q, l, "B1")]
    prefetch_next(q, l, "B1")
    wB = V3(slots[s][:, 0:8 * 2576], 8)
    def dtmm(e):
        for t in range(NCQ):
            for kk in range(8):
                ins = e.matmul(bank(0)[:, t * 16:(t + 1) * 16], lhsT=hT3[:, kk, t * 128:(t + 1) * 128], rhs=wB[:, kk, 2560:2576],
                               start=(kk == 0), stop=(kk == 7))
        return ins
    P.op("pe", dtmm, reads=HTK + SK(s), writes=PK(0))
    dt_all = small[:, 64:128]
    a_all = small[:, 128:192]
    eacs = small[:, 192:256]
    dstt = small[:, 256:320]
    cdq = small[:, 320:384]
    dth = small[:, 384:448]
    P.op("dve", lambda e: e.tensor_tensor(out=V3(dt_all, 4), in0=V3(bank(0)[:, 0:64], 4),
                                          in1=dtb[l][:, :].unsqueeze(1).to_broadcast([128, 4, 16]), op=ALU.add),
         reads=PK(0) + [("dtb", l)], writes=["dt_all"])
    P.op("dve", lambda e: e.tensor_scalar(out=dt_all, in0=dt_all, scalar1=30.0, scalar2=None, op0=ALU.min), reads=["dt_all"],
         writes=["dt_all"])
    P.op("act", lambda e: e.activation(out=dt_all, in_=dt_all, func=AF.Exp), reads=["dt_all"], writes=["dt_all"])
    P.op("act", lambda e: e.activation(out=dt_all, in_=dt_all, func=AF.Ln, bias=1.0), reads=["dt_all"], writes=["dt_all"])
    P.op("dve", lambda e: e.tensor_tensor(out=V3(a_all, 4), in0=V3(dt_all, 4),
                                          in1=A_bc[l][:, :].unsqueeze(1).to_broadcast([128, 4, 16]), op=ALU.mult),
         reads=["dt_all", ("A_bc", l)], writes=["a_all"])
    xp3 = V3(xpad[:, :], 12)
    xc3 = V3(xc[:, :], 12)
    xt3 = V3(xtail[l][:, :], 12)
    ssd16 = V3(ssd_f[l][:, :], 16)
    tx = W[0][:, 0:768].bitcast(BF16)
    eseg = W[0][:, :].bitcast(BF16)
    Ybf = W[1][:, :].bitcast(BF16)
    MTb = W[2][:, :].bitcast(BF16)
    MT = V3(MTb, 16)
    xdt = W[3][:, 0:512].bitcast(BF16)
    xdtd = W[3][:, 512:1024].bitcast(BF16)
    y1 = W[4][:, :]
    sz = W[5][:, 0:512].bitcast(BF16)
    yn = W[5][:, 512:1024].bitcast(BF16)
    btok = V3(misc1[:, 256:512], 2)
    Gm = V3(misc1[:, 0:256], 2)
    xsT = V3(bankb(2), 16)
    ssy, msy, rsy = small[:, 40:41], small[:, 41:42], small[:, 42:43]
    tzb = W[4][:, :]

    def b1_proj(t):
        tok = slice(t * 128, (t + 1) * 128)
        P.op("pool", lambda e: e.tensor_tensor(
            out=V3(Ybf, 16), in0=a_all[:, t * 16:(t + 1) * 16].unsqueeze(2).to_broadcast([128, 16, 128]),
            in1=Uf.unsqueeze(1).to_broadcast([128, 16, 128]), op=ALU.mult),
            reads=["a_all", "cf", *wk(4)], writes=[*wk(1)])
        P.op("act", lambda e: e.activation(out=xp3[:, :, 0:3], in_=xt3, func=AF.Copy), reads=[("xtail", l)], writes=["xpad_h"])

        def zp(e):
            for n in range(2):
                for kk in range(8):
                    ins = e.matmul(bank(n), lhsT=hT3[:, kk, tok], rhs=wB[:, kk, n * 512:(n + 1) * 512], start=(kk == 0),
                                   stop=(kk == 7))
            return ins
        P.op("pe", zp, reads=HTK + SK(s), writes=PK(0, 1))
        for j in range(3):
            def xp(e, j=j):
                for m in range(4 * j, 4 * j + 4):
                    for kk in range(8):
                        ins = e.matmul(ps[:, 1024 + m * 128:1024 + (m + 1) * 128],
                                       lhsT=wB[:, kk, 1024 + m * 128:1024 + (m + 1) * 128],
                                       rhs=hT3[:, kk, tok], start=(kk == 0), stop=(kk == 7))
                return ins
            P.op("pe", xp, reads=HTK + SK(s), writes=PK(2 + j))
            P.op("act", lambda e, j=j: e.activation(out=xp3[:, 4 * j:4 * j + 4, 3:131], in_=V3(bank(2 + j), 4), func=AF.Copy),
                 reads=PK(2 + j), writes=[("xpad_b", j)])
        P.op("act", lambda e: e.activation(out=xt3, in_=xp3[:, :, 128:131], func=AF.Copy),
             reads=[("xpad_b", j) for j in range(3)], writes=[("xtail", l)])

    def b1_front(t):
        P.op("act", lambda e: e.activation(out=sz, in_=ps[:, 0:1024], func=AF.Tanh, scale=0.5), reads=PK(0, 1), writes=[*wk(5,0,512)])
        P.op("dve", lambda e: e.scalar_tensor_tensor(out=sz, in0=sz, scalar=1.0, in1=ps[:, 0:1024], op0=ALU.add, op1=ALU.mult),
             reads=[*wk(5,0,512)] + PK(0, 1), writes=[*wk(5,0,512)])
        for j in range(3):
            def conv(e, j=j):
                for m in range(4 * j, 4 * j + 4):
                    for k in range(4):
                        e.matmul(ps[:, 2560 + m * 128:2560 + (m + 1) * 128], lhsT=dg4[:, m, k, :], rhs=xp3[:, m, k:k + 128],
                                 start=(k == 0), stop=False)
                    ins = e.matmul(ps[:, 2560 + m * 128:2560 + (m + 1) * 128], lhsT=cbrow[0:1, m * 128:(m + 1) * 128],
                                   rhs=onesrow[0:1, :], start=False, stop=True)
                return ins
            P.op("pe", conv, reads=["xpad_h", ("xpad_b", j), "diagw", "cbrow", "onesrow"], writes=PK(5 + j))
            txj = tx[:, j * 512:(j + 1) * 512]
            txk = wk(0, j * 256, j * 256 + 256)
            P.op("act", lambda e, j=j, txj=txj: e.activation(out=txj, in_=bank(5 + j), func=AF.Tanh, scale=0.5), reads=PK(5 + j),
                 writes=txk)
            P.op("dve", lambda e, j=j, txj=txj: e.scalar_tensor_tensor(out=xc[:, j * 512:(j + 1) * 512], in0=txj, scalar=1.0,
                                                                       in1=bank(5 + j), op0=ALU.add, op1=ALU.mult),
                 reads=txk + PK(5 + j), writes=[("xc", j)])
        if t >= 1:
            b1_gate_pe(t - 1)

        def trx(e):
            for m in range(8):
                ins = e.transpose(out=bankb(2)[:, m * 128:(m + 1) * 128], in_=xc3[:, m, :], identity=identb)
            return ins
        P.op("pe", trx, reads=[("xc", 0), ("xc", 1), "cb16"], writes=PK(2))

        def trb(e):
            for gI in range(2):
                e.transpose(out=bankb(3)[:, gI * 128:(gI + 1) * 128], in_=xc3[:, 8 + gI, :], identity=identb)
            for gI in range(2):
                ins = e.matmul(ps[:, 1792 + gI * 128:1792 + (gI + 1) * 128], lhsT=xc3[:, 8 + gI, :], rhs=xc3[:, 10 + gI, :],
                               start=True, stop=True)
            return ins
        P.op("pe", trb, reads=[("xc", 2), "cb16"], writes=PK(3))

        def segmm(e):
            for n in range(4):
                ins = e.matmul(bank(4 + n), lhsT=xgt, rhs=Ybf[:, n * 512:(n + 1) * 512], start=True, stop=True)
            return ins
        P.op("pe", segmm, reads=[*wk(1), "cb16"], writes=PK(4, 5, 6, 7))
        P.op("dve", lambda e: e.tensor_tensor(out=Gm, in0=V3(ps[:, 1792:2048], 2),
                                              in1=cmask.unsqueeze(1).to_broadcast([128, 2, 128]), op=ALU.mult),
             reads=PK(3) + ["cf"], writes=["Gm"])
        P.op("dve", lambda e: e.tensor_tensor(out=V3(xdt, 16), in0=xsT,
                                              in1=dth[:, t * 16:(t + 1) * 16].unsqueeze(2).to_broadcast([128, 16, 64]),
                                              op=ALU.mult), reads=PK(2) + ["dth"], writes=[*wk(3,0,512)])
        for hg in range(2):
            P.op("act", lambda e, hg=hg: e.activation(out=eseg[:, hg * 1024:(hg + 1) * 1024], in_=ps[:, 2048 + hg * 1024:3072 + hg * 1024],
                                                       func=AF.Exp), reads=PK(4 + 2 * hg, 5 + 2 * hg), writes=wk(0, hg * 512, hg * 512 + 512))
            P.op("dve", lambda e, hg=hg: e.tensor_tensor(out=V3(MTb[:, hg * 1024:(hg + 1) * 1024], 8),
                                                         in0=V3(eseg[:, hg * 1024:(hg + 1) * 1024], 8),
                                                         in1=Gm[:, hg:hg + 1, :].to_broadcast([128, 8, 128]), op=ALU.mult),
                 reads=wk(0, hg * 512, hg * 512 + 512) + ["Gm"], writes=wk(2, hg * 512, hg * 512 + 512))
        P.op("dve", lambda e: e.tensor_tensor(out=V3(xdtd, 16), in0=xsT,
                                              in1=dstt[:, t * 16:(t + 1) * 16].unsqueeze(2).to_broadcast([128, 16, 64]),
                                              op=ALU.mult), reads=PK(2) + ["dstt"], writes=[*wk(3,512,1024)])
        P.op("act", lambda e: e.activation(out=misc1[:, 256:512], in_=bankb(3)[:, 0:256], func=AF.Copy), reads=PK(3),
             writes=["btok"])
        if D3 and t == 0:
            dump("xc", xc[:, :], 1536, BF16, reads=[("xc", 0), ("xc", 1), ("xc", 2)])

    def b1_scan(t):
        def ydiag(e, bk):
            if True:
                for h in range(bk * 8, bk * 8 + 8):
                    e.matmul(ps[:, 2048 + h * 64:2048 + (h + 1) * 64], lhsT=MT[:, h, :], rhs=xdt[:, h * 64:(h + 1) * 64],
                             start=(h == bk * 8), stop=False, skip_group_check=True)
                for m in range(bk * 4, bk * 4 + 4):
                    ins = e.matmul(ps[:, 2048 + m * 128:2048 + (m + 1) * 128], lhsT=xc3[:, m, :], rhs=dm3[:, m, :],
                                   start=False, stop=True, skip_group_check=True)
            return ins
        for bk in range(2):
            P.op("pe", lambda e, bk=bk: ydiag(e, bk), reads=wk(2, bk * 512, bk * 512 + 512) + [*wk(3,0,512), ("xc", 0), ("xc", 1), "dmt"],
                 writes=PK(4 + bk))

        def yoff(e):
            for gI in range(2):
                ins = e.matmul(bank(6 + gI), lhsT=xc3[:, 10 + gI, :], rhs=ssd_bf[:, gI * 512:(gI + 1) * 512], start=True, stop=True)
            return ins
        P.op("pe", yoff, reads=[("xc", 2), "ssd_bf"], writes=PK(6, 7))

        def smm(e):
            for gI in range(2):
                ins = e.matmul(ps[:, 1024 + gI * 512:1024 + (gI + 1) * 512], lhsT=btok[:, gI, :],
                               rhs=xdtd[:, gI * 512:(gI + 1) * 512], start=True, stop=True)
            return ins
        P.op("pe", smm, reads=["btok", *wk(3,512,1024)], writes=PK(2, 3))
        P.op("dve", lambda e: e.tensor_tensor(out=V3(y1, 16), in0=V3(ps[:, 3072:4096], 16),
                                              in1=eacs[:, t * 16:(t + 1) * 16].unsqueeze(2).to_broadcast([128, 16, 64]),
                                              op=ALU.mult), reads=PK(6, 7) + ["eacs"], writes=[*wk(4)])
        P.op("dve", lambda e: e.tensor_tensor(out=y1, in0=y1, in1=ps[:, 2048:3072], op=ALU.add), reads=[*wk(4)] + PK(4, 5),
             writes=[*wk(4)])
        P.op("dve", lambda e: e.tensor_tensor(out=ssd16, in0=ssd16,
                                               in1=cdq[:, t * 16:(t + 1) * 16].unsqueeze(2).to_broadcast([128, 16, 64]),
                                               op=ALU.mult), reads=[("ssd_f", l), "cdq"], writes=[("ssd_f", l)])
        P.op("dve", lambda e: e.tensor_tensor(out=ssd_f[l][:, :], in0=ssd_f[l][:, :], in1=ps[:, 1024:2048], op=ALU.add),
             reads=[("ssd_f", l)] + PK(2, 3), writes=[("ssd_f", l)])
        P.op("act", lambda e: e.activation(out=ssd_bf[:, :], in_=ssd_f[l][:, :], func=AF.Copy), reads=[("ssd_f", l)],
             writes=["ssd_bf"])
        if D3 and t == 0:
            dump("y1a", y1, 1024, reads=[*wk(4)])

    def b1_gate(t):
        P.op("dve", lambda e: e.tensor_tensor(out=y1, in0=y1, in1=sz, op=ALU.mult), reads=[*wk(4), *wk(5,0,512)], writes=[*wk(4)])
        P.op("act", lambda e: e.activation(out=yn, in_=y1, func=AF.Square, accum_out=ssy), reads=[*wk(4)], writes=[*wk(5,512,1024), "ssy"])
        P.op("dve", lambda e: e.tensor_scalar(out=msy, in0=ssy, scalar1=1.0 / 1024, scalar2=4.0 * EPS, op0=ALU.mult,
                                              op1=ALU.add), reads=["ssy"], writes=["msy"])
        rsqrt_small(rsy, msy, 1, ["msy"], ["rsy"])
        P.op("dve", lambda e: e.scalar_tensor_tensor(out=yn, in0=y1, scalar=rsy, in1=gbc[:, :], op0=ALU.mult, op1=ALU.mult),
             reads=[*wk(4), "rsy", "gbc"], writes=[*wk(5,512,1024)])
        if D3 and t == 0:
            dump("y1", y1, 1024, reads=[*wk(4)])
            dump("yn", yn, 1024, BF16, reads=[*wk(5,512,1024)])

    def b1_gate_pe(t, bk=0):
        tok = slice(t * 128, (t + 1) * 128)

        def tryn(e):
            for m in range(8):
                ins = e.transpose(out=bankb(bk)[:, m * 128:(m + 1) * 128], in_=yn[:, m * 128:(m + 1) * 128], identity=identb)
            return ins
        P.op("pe", tryn, reads=[*wk(5,512,1024), "cb16"], writes=PK(bk))
        P.op("act", lambda e: e.activation(out=yT3[:, :, tok], in_=V3(bankb(bk), 8), func=AF.Copy), reads=PK(bk),
             writes=["ytemp"])

    def dt_b():
        P.op("pe", lambda e: e.matmul(bank(7)[:, 0:64], lhsT=Uf, rhs=a_all, start=True, stop=True), reads=["a_all", "cf"],
             writes=PK(7))
        P.op("pe", lambda e: e.matmul(bank(7)[:, 64:128], lhsT=onesf, rhs=a_all, start=True, stop=True), reads=["a_all", "cf"],
             writes=PK(7))
        P.op("act", lambda e: e.activation(out=eacs, in_=bank(7)[:, 0:64], func=AF.Exp, bias=math.log(0.25)), reads=PK(7),
             writes=["eacs"])
        P.op("act", lambda e: e.activation(out=cdq, in_=bank(7)[:, 64:128], func=AF.Exp), reads=PK(7), writes=["cdq"])
        acs_sb = small[:, 448:512]
        P.op("act", lambda e: e.activation(out=acs_sb, in_=bank(7)[:, 0:64], func=AF.Copy), reads=PK(7), writes=["acs_sb"])
        P.op("dve", lambda e: e.tensor_tensor(out=dstt, in0=bank(7)[:, 64:128], in1=acs_sb, op=ALU.subtract),
             reads=PK(7) + ["acs_sb"], writes=["dstt"])
        P.op("act", lambda e: e.activation(out=dstt, in_=dstt, func=AF.Exp), reads=["dstt"], writes=["dstt"])
        P.op("dve", lambda e: e.tensor_scalar(out=dth, in0=dt_all, scalar1=0.5, scalar2=None, op0=ALU.mult), reads=["dt_all"],
             writes=["dth"])
        P.op("dve", lambda e: e.tensor_tensor(out=dstt, in0=dstt, in1=dth, op=ALU.mult), reads=["dstt", "dth"], writes=["dstt"])
        if D3:
            dump("small", small[:, :], 512, reads=["dt_all", "a_all", "eacs", "dstt", "cdq", "dth"])


    for t in range(NCQ):
        b1_proj(t)
        if t == NCQ - 1:
            prefetch_next(q, l, "B2", early=True)
        if t == 0:
            dt_b()
        if t >= 1:
            b1_gate(t - 1)
        b1_front(t)
        b1_scan(t)

    def b1_last():
        b1_gate(NCQ - 1)

    P.mark("B1")
    merge_pass("B2", 8, False, False, inject_early=b1_last, inject=lambda: b1_gate_pe(NCQ - 1, bk=7))
    P.mark("B2")

    ds4 = V4(diagsc[:, 0:12 * 128], 4, 3)
    P.op("pool", lambda e: e.tensor_tensor(out=V3(diagsc[:, 0:12 * 128], 12), in0=identb.unsqueeze(1).to_broadcast([128, 12, 128]),
                                           in1=scw[l][:, :].unsqueeze(2).to_broadcast([128, 12, 128]), op=ALU.mult),
         reads=["cb16", ("scw", l)], writes=["diagw"])

    if l == n_layers_ - 1 and q + 1 < n_quarters_:
        emit_rotary(q + 1)
    s = slot_of[(q, l, "C1")]
    prefetch_next(q, l, "C1")
    wC = V3(slots[s][:, 0:8 * 2048], 8)
    up3 = V3(xpad[:, 0:4 * 130], 4)
    ut3 = V3(utail[l][:, :], 4)
    def c1_proj(t):
        tok = slice(t * 128, (t + 1) * 128)
        b0 = 4 * (t % 2)

        def cp(e):
            for m in range(16):
                for kk in range(8):
                    ins = e.matmul(ps[:, b0 * 512 + m * 128:b0 * 512 + (m + 1) * 128], lhsT=wC[:, kk, m * 128:(m + 1) * 128],
                                   rhs=hT3[:, kk, tok], start=(kk == 0), stop=(kk == 7))
            return ins
        P.op("pe", cp, reads=HTK + SK(s), writes=PK(b0, b0 + 1, b0 + 2, b0 + 3))

    c1_proj(0)
    for t in range(NCQ):
        tok = slice(t * 128, (t + 1) * 128)
        par = t % 2
        b0 = 4 * par
        if t + 1 < NCQ:
            c1_proj(t + 1)
            if t + 1 == NCQ - 1:
                prefetch_next(q, l, "C2", early=True)
        chs = W[0][:, par * 512:par * 512 + 256].bitcast(BF16)
        chk = wk(0, par * 512, par * 512 + 256)
        tcg = W[0][:, par * 512 + 256:par * 512 + 512].bitcast(BF16)
        tck = wk(0, par * 512 + 256, par * 512 + 512)
        s1 = W[1][:, par * 512:(par + 1) * 512]
        s1k = wk(1, par * 512, par * 512 + 512)
        P.op("act", lambda e, chs=chs, b0=b0: e.activation(out=chs, in_=bank(b0 + 2), func=AF.Copy), reads=PK(b0 + 2), writes=chk)
        P.op("act", lambda e, tcg=tcg, b0=b0: e.activation(out=tcg, in_=bank(b0 + 3), func=AF.Tanh, scale=0.5), reads=PK(b0 + 3),
             writes=tck)
        P.op("act", lambda e: e.activation(out=up3[:, :, 0:2], in_=ut3, func=AF.Copy), reads=[("utail", l)], writes=["xpad_h", ("xpad_b", 0), ("xpad_b", 1), ("xpad_b", 2)])
        P.op("dve", lambda e, chs=chs, b0=b0: e.tensor_tensor(out=up3[:, :, 2:130], in0=V3(bank(b0 + 1), 4), in1=V3(chs, 4), op=ALU.mult),
             reads=PK(b0 + 1) + chk, writes=["xpad_h", ("xpad_b", 0), ("xpad_b", 1), ("xpad_b", 2)])
        P.op("act", lambda e: e.activation(out=ut3, in_=up3[:, :, 128:130], func=AF.Copy), reads=["xpad_h", ("xpad_b", 0), ("xpad_b", 1), ("xpad_b", 2)],
             writes=[("utail", l)])

        def cconv(e, b0=b0):
            for m in range(4):
                for k in range(3):
                    ins = e.matmul(bank(b0 + 1)[:, m * 128:(m + 1) * 128], lhsT=ds4[:, m, k, :], rhs=up3[:, m, k:k + 128],
                                   start=(k == 0), stop=(k == 2))
            return ins
        P.op("pe", cconv, reads=["xpad_h", ("xpad_b", 0), ("xpad_b", 1), ("xpad_b", 2), "diagw"], writes=PK(b0 + 1))
        P.op("dve", lambda e, s1=s1, tcg=tcg, b0=b0: e.scalar_tensor_tensor(out=s1, in0=tcg, scalar=1.0, in1=bank(b0 + 3), op0=ALU.add,
                                                                            op1=ALU.mult), reads=tck + PK(b0 + 3), writes=s1k)
        P.op("dve", lambda e, s1=s1, b0=b0: e.tensor_tensor(out=s1, in0=s1, in1=bank(b0), op=ALU.mult), reads=s1k + PK(b0), writes=s1k)
        P.op("dve", lambda e, tok=tok, s1=s1, b0=b0: e.tensor_tensor(out=yT3[:, 0:4, tok], in0=V3(s1, 4), in1=V3(bank(b0 + 1), 4),
                                                                     op=ALU.mult), reads=s1k + PK(b0 + 1), writes=["ytemp"])

    P.mark("C1")
    if l + 1 < n_layers_:
        emit_prep(l + 1)
    elif q + 1 < n_quarters_:
        emit_prep(0)
    is_last_layer = (l == g["n_layers"] - 1)
    if is_last_layer:
        P.dma("sp", lambda e: e.dma_start(out=gbc[:, :], in_=g["fg_d"]), writes=["gbc"])
    merge_pass("C2", 4, False, True, after_tail=(g["final_cb"] if is_last_layer else None))
    if D3:
        dump("mgC", merged[:, 0:1024], 1024, BF16, reads=[("mg", 0, 0), ("mg", 0, 1)])
        dump("xq0", xq[:, 0:1024], 1024, reads=[("xq", 0)])


def _consts():
    cf = np.zeros((128, C_TOT), np.float32)
    idx = np.arange(128)
    cf[:, C_U:C_U + 128] = (idx[:, None] <= idx[None, :])
    cf[:, C_ONES:C_ONES + 128] = 1.0
    cf[:, C_CMASK:C_CMASK + 128] = 0.25 * (idx[None, :] >= idx[:, None])
    gam = 1.0 - np.exp2(-5.0 - np.arange(4))
    for h in range(4):
        lg = math.log(gam[h])
        m = np.where(idx[None, :] >= idx[:, None], np.exp(-(idx[:, None] + 1.0) * lg), 0.0)
        cf[:, C_MASKT + h * 128:C_MASKT + (h + 1) * 128] = m
        cf[:, C_QDEC + h * 128:C_QDEC + (h + 1) * 128] = (np.exp((idx + 1.0) * lg) * 128 ** -0.5)[None, :]
        cf[:, C_KDEC + h] = np.exp((127.0 - idx) * lg)
    cf[:, C_INVF:C_INVF + 64] = (10000.0 ** (-np.arange(0, 128, 2, dtype=np.float64) / 128))[None, :]
    cf[:, C_NHALF:C_NHALF + 16] = -0.5
    cf[:, C_TWOPI:C_TWOPI + 1] = 2.0 * math.pi
    cb = np.zeros((128, 256), np.float32)
    cb[:, 0:128] = np.eye(128)
    cb[:, 128:256] = (idx[:, None] > idx[None, :])
    return cf, cb.astype(ml_dtypes.bfloat16)


_PROGRAM = {}


def _get_program(dbg=False, n_quarters=NQ, n_layers=DEPTH):
    key = (dbg, n_quarters, n_layers)
    if key not in _PROGRAM:
        _PROGRAM[key] = build_program(dbg, n_quarters, n_layers)
    return _PROGRAM[key]


def make_in_maps(x, c, positions, norm_g, w_ada, b_ada, w_in, ssm_conv_w, ssm_conv_b, ssm_dt_bias, ssm_a_log, ssm_d,
                 ssm_norm_g, sc_conv_w, w_br_ret, w_br_ssm, w_br_sc, w_out, final_norm_g):
    f = lambda a: np.ascontiguousarray(np.asarray(a, dtype=np.float32))
    cf, cb = _consts()
    B = x.shape[0]
    rep = lambda a: np.ascontiguousarray(np.broadcast_to(np.asarray(a, np.float32)[:, None, :], (DEPTH, 128, a.shape[-1])))
    shared = {
        "w_in": f(w_in), "w_ada": f(w_ada), "w_br_ret": f(w_br_ret), "w_br_ssm": f(w_br_ssm), "w_br_sc": f(w_br_sc),
        "w_out": f(w_out),
        "norm_g_fm": f(np.asarray(norm_g).reshape(DEPTH, 8, 128).transpose(0, 2, 1)),
        "b_ada_bc": rep(np.asarray(b_ada)),
        "conv_w_fm": f(np.asarray(ssm_conv_w).reshape(DEPTH, 4, 12, 128).transpose(0, 3, 2, 1).reshape(DEPTH, 128, 48)),
        "conv_b_row": f(np.asarray(ssm_conv_b).reshape(DEPTH, 1, 1536)),
        "dt_bias_bc": rep(np.asarray(ssm_dt_bias)),
        "a_log_bc": rep(np.asarray(ssm_a_log)),
        "d_fm": f(np.repeat(np.asarray(ssm_d), 64, axis=1).reshape(DEPTH, 8, 128).transpose(0, 2, 1)),
        "ssm_norm_g_bc": rep(np.asarray(ssm_norm_g)),
        "sc_w_fm": f(np.asarray(sc_conv_w).reshape(DEPTH, 3, 4, 128).transpose(0, 3, 2, 1).reshape(DEPTH, 128, 12)),
        "final_g_bc": np.ascontiguousarray(np.broadcast_to(np.asarray(final_norm_g, np.float32)[None, :], (128, D))),
        "cf32": cf, "cbf16": cb,
    }
    maps = []
    for b in range(B):
        m = dict(shared)
        m["x"] = f(x[b])
        m["c"] = f(np.asarray(c[b]).reshape(8, 128).T)
        m["pos"] = np.ascontiguousarray(np.asarray(positions[b], np.int32).reshape(16, 128).T)
        maps.append(m)
    return maps


def kernel(**inputs):
    maps = make_in_maps(**inputs)
    nc = _get_program()
    res = run_bass_kernel_spmd(nc, maps, core_ids=list(range(len(maps))))
    return np.stack([np.asarray(r["out"], dtype=np.float32) for r in res.results], axis=0)
```
